# Optimizing a Trainium2 kernel written in Bass

```python
import math
import jax, jax.numpy as jnp
from jax import lax
import numpy as np

D_MODEL = 1024
BATCH = 8
SEQ = 2048
DEPTH = 2
DEC_BATCH = 128
DEC_SEQ = 8
PAST_LEN = 16384
PAGE_SIZE = 128

N_META = 16
C_CONV = D_MODEL
CONF_W = 31
D_INNER = 2 * D_MODEL
M_HEADDIM = 64
M_HEADS = D_INNER // M_HEADDIM
M_GROUPS = 8
HPG = M_HEADS // M_GROUPS
M_STATE = 128
M_CONV_W = 4
CONV_DIM = D_INNER + 2 * M_GROUPS * M_STATE
SSD_CHUNK = 128
N_IN = 2 * C_CONV + D_INNER + CONV_DIM + M_HEADS + 2 * D_MODEL
PEER_HEADS = 8
N_KEYS = 128
N_EXPERTS = N_KEYS * N_KEYS
PEER_TOPK = 16
D_KEY = 256
PEER_BLOCK = 256
EPS = 1e-6

kernel_name = "hybrid_conformer_ssd_peer_step"


def rmsnorm(x, w):
    xf = x.astype(jnp.float32)
    y = xf * lax.rsqrt(jnp.mean(xf * xf, axis=-1, keepdims=True) + EPS)
    return (y * w.astype(jnp.float32)).astype(x.dtype)


def layernorm(x, g, b):
    xf = x.astype(jnp.float32)
    mu = jnp.mean(xf, axis=-1, keepdims=True)
    xc = xf - mu
    y = xc * lax.rsqrt(jnp.mean(xc * xc, axis=-1, keepdims=True) + EPS)
    return (y * g.astype(jnp.float32) + b.astype(jnp.float32)).astype(x.dtype)


def gated_group_rmsnorm(y, z, w):
    yf = y.astype(jnp.float32) * jax.nn.silu(z.astype(jnp.float32))
    yg = yf.reshape(*y.shape[:-1], M_GROUPS, D_INNER // M_GROUPS)
    yg = yg * lax.rsqrt(jnp.mean(yg * yg, axis=-1, keepdims=True) + EPS)
    return (yg.reshape(y.shape) * w.astype(jnp.float32)).astype(y.dtype)


def causal_depthwise(x_cat, w, b):
    c = x_cat.shape[-1]
    y = lax.conv_general_dilated(x_cat, w[:, None, :].astype(x_cat.dtype), (1,), 'VALID',
                                 dimension_numbers=('NWC', 'WIO', 'NWC'), feature_group_count=c)
    return y + b.astype(y.dtype)


def ssd_segment(x, dt, A, Bm, Cm, h, chunk):
    f32 = jnp.float32
    b, T = x.shape[:2]
    nc = T // chunk
    xf = x.astype(f32).reshape(b, nc, chunk, M_GROUPS, HPG, M_HEADDIM)
    dtf = dt.astype(f32).reshape(b, nc, chunk, M_GROUPS, HPG)
    Bf = Bm.astype(f32).reshape(b, nc, chunk, M_GROUPS, M_STATE)
    Cf = Cm.astype(f32).reshape(b, nc, chunk, M_GROUPS, M_STATE)
    Ag = A.astype(f32).reshape(M_GROUPS, HPG)
    causal = jnp.tril(jnp.ones((chunk, chunk), dtype=bool))
    h0 = h.astype(f32).reshape(b, M_GROUPS, HPG, M_HEADDIM, M_STATE)

    def step(hc, inp):
        xc, dtc, Bc, Cc = inp
        acs = jnp.cumsum(dtc * Ag, axis=1)
        seg = acs[:, :, None] - acs[:, None, :]
        decay = jnp.exp(jnp.where(causal[None, :, :, None, None], seg, -jnp.inf))
        cb = jnp.einsum('btgn,bsgn->btsg', Cc, Bc)
        dx = dtc[..., None] * xc
        y_diag = jnp.einsum('btsgr,bsgrp->btgrp', cb[..., None] * decay, dx)
        y_off = jnp.einsum('btgn,bgrpn->btgrp', Cc, hc) * jnp.exp(acs)[..., None]
        last = acs[:, -1]
        w_end = jnp.exp(last[:, None] - acs)
        st = jnp.einsum('bsgn,bsgrp->bgrpn', Bc, w_end[..., None] * dx)
        h_new = jnp.exp(last)[..., None, None] * hc + st
        return h_new, y_diag + y_off

    inp = tuple(jnp.moveaxis(a, 1, 0) for a in (xf, dtf, Bf, Cf))
    hT, ys = lax.scan(step, h0, inp)
    y = jnp.moveaxis(ys, 0, 1).reshape(b, T, M_HEADS, M_HEADDIM)
    return y.astype(x.dtype), hT.reshape(h.shape).astype(h.dtype)


def peer_block(xb, w_q, keys, U, V):
    T = xb.shape[0]
    q = (xb @ w_q).reshape(T, PEER_HEADS, 2, D_KEY // 2)
    s = jnp.einsum('thcd,hckd->thck', q, keys).astype(jnp.float32)
    s1, i1 = lax.top_k(s[:, :, 0], PEER_TOPK)
    s2, i2 = lax.top_k(s[:, :, 1], PEER_TOPK)
    cand = (s1[..., :, None] + s2[..., None, :]).reshape(T, PEER_HEADS, PEER_TOPK * PEER_TOPK)
    sc, f = lax.top_k(cand, PEER_TOPK)
    ids = (jnp.take_along_axis(i1, f // PEER_TOPK, axis=-1) * N_KEYS
           + jnp.take_along_axis(i2, f % PEER_TOPK, axis=-1))
    g = jax.nn.softmax(sc, axis=-1)
    act = jax.nn.gelu(jnp.einsum('thkd,td->thk', U[ids], xb).astype(jnp.float32), approximate=False)
    w = (g * act).astype(xb.dtype)
    return jnp.einsum('thk,thkd->td', w, V[ids])


def peer(xn, w_q, keys, U, V):
    b, T, D = xn.shape
    n = b * T
    pad = (-n) % PEER_BLOCK
    xt = jnp.pad(xn.reshape(n, D), ((0, pad), (0, 0)))
    out = lax.map(lambda xb: peer_block(xb, w_q, keys, U, V), xt.reshape(-1, PEER_BLOCK, D))
    return out.reshape(-1, D)[:n].reshape(b, T, D)


def trunk_layer(x, conf_buf, mconv_buf, h0, segments, p):
    (norm1_w, w_in, b_in, conf_dw_w, conf_dw_b, conf_ln_g, conf_ln_b, w_conf_out, b_conf_out,
     m_conv_w, m_conv_b, dt_bias, A_log, D_skip, m_norm_w, w_m_out, w_o, norm2_w,
     peer_w_q, peer_keys, peer_u, peer_v) = p
    b, T, _ = x.shape
    xn = rmsnorm(x, norm1_w)
    proj = xn @ w_in + b_in
    cuts = np.cumsum([2 * C_CONV, D_INNER, CONV_DIM, M_HEADS])
    glu_in, z, xbc, dt_raw, gate_raw = jnp.split(proj, [int(c) for c in cuts], axis=-1)

    ga, gb = jnp.split(glu_in, 2, axis=-1)
    u = ga * jax.nn.sigmoid(gb)
    conf_cat = jnp.concatenate([conf_buf.astype(u.dtype), u], axis=1)
    new_conf = conf_cat[:, -(CONF_W - 1):]
    c = causal_depthwise(conf_cat, conf_dw_w, conf_dw_b)
    c = jax.nn.silu(layernorm(c, conf_ln_g, conf_ln_b))
    y_a = c @ w_conf_out + b_conf_out

    m_cat = jnp.concatenate([mconv_buf.astype(xbc.dtype), xbc], axis=1)
    new_mconv = m_cat[:, -(M_CONV_W - 1):]
    xbc_c = jax.nn.silu(causal_depthwise(m_cat, m_conv_w, m_conv_b))
    xs, Bm, Cm = jnp.split(xbc_c, [D_INNER, D_INNER + M_GROUPS * M_STATE], axis=-1)
    xs = xs.reshape(b, T, M_HEADS, M_HEADDIM)
    Bm = Bm.reshape(b, T, M_GROUPS, M_STATE)
    Cm = Cm.reshape(b, T, M_GROUPS, M_STATE)
    dt = jax.nn.softplus(dt_raw.astype(jnp.float32) + dt_bias.astype(jnp.float32))
    A = -jnp.exp(A_log.astype(jnp.float32))
    ys, h, off = [], h0, 0
    for length, chunk in segments:
        y_seg, h = ssd_segment(xs[:, off:off + length], dt[:, off:off + length], A,
                               Bm[:, off:off + length], Cm[:, off:off + length], h, chunk)
        ys.append(y_seg)
        off += length
    y = jnp.concatenate(ys, axis=1) + D_skip[:, None].astype(xs.dtype) * xs
    y = gated_group_rmsnorm(y.reshape(b, T, D_INNER), z, m_norm_w)
    y_b = y @ w_m_out

    g_a, g_b = jnp.split(jax.nn.sigmoid(gate_raw), 2, axis=-1)
    x = x + (g_a * y_a + g_b * y_b) @ w_o

    x = x + peer(rmsnorm(x, norm2_w), peer_w_q, peer_keys, peer_u, peer_v)
    return x, new_conf, new_mconv, h


def setup_inputs(seed: int = 0) -> dict:
    key = jax.random.key(seed)
    ks = jax.random.split(key, 32)
    nrm = lambda k, shape, s: jax.random.normal(k, shape, jnp.float32) * s
    dt0 = jnp.exp(jax.random.uniform(ks[14], (DEPTH, M_HEADS), jnp.float32, math.log(1e-3), math.log(1e-1)))
    return {
        "x_prompt": nrm(ks[0], (BATCH, SEQ, D_MODEL), 1.0),
        "x_sample": nrm(ks[1], (DEC_BATCH, DEC_SEQ, D_MODEL), 1.0),
        "cache_conf": nrm(ks[2], (DEPTH, DEC_BATCH, CONF_W - 1, C_CONV), 0.5),
        "cache_mconv": nrm(ks[3], (DEPTH, DEC_BATCH, M_CONV_W - 1, CONV_DIM), 0.5),
        "state_ssm": nrm(ks[4], (DEPTH, DEC_BATCH, M_HEADS, M_HEADDIM, M_STATE), 0.1),
        "meta_tokens": nrm(ks[5], (N_META, D_MODEL), 1.0),
        "norm1_w": 1.0 + nrm(ks[6], (DEPTH, D_MODEL), 0.02),
        "w_in": nrm(ks[7], (DEPTH, D_MODEL, N_IN), D_MODEL ** -0.5),
        "b_in": nrm(ks[8], (DEPTH, N_IN), 0.02),
        "conf_dw_w": nrm(ks[9], (DEPTH, CONF_W, C_CONV), CONF_W ** -0.5),
        "conf_dw_b": nrm(ks[10], (DEPTH, C_CONV), 0.02),
        "conf_ln_g": 1.0 + nrm(ks[11], (DEPTH, C_CONV), 0.02),
        "conf_ln_b": nrm(ks[12], (DEPTH, C_CONV), 0.02),
        "w_conf_out": nrm(ks[13], (DEPTH, C_CONV, D_MODEL), C_CONV ** -0.5),
        "b_conf_out": nrm(ks[15], (DEPTH, D_MODEL), 0.02),
        "m_conv_w": nrm(ks[16], (DEPTH, M_CONV_W, CONV_DIM), M_CONV_W ** -0.5),
        "m_conv_b": nrm(ks[17], (DEPTH, CONV_DIM), 0.02),
        "dt_bias": dt0 + jnp.log(-jnp.expm1(-dt0)),
        "A_log": jnp.log(jax.random.uniform(ks[18], (DEPTH, M_HEADS), jnp.float32, 1.0, 16.0)),
        "D_skip": 1.0 + nrm(ks[19], (DEPTH, M_HEADS), 0.02),
        "m_norm_w": 1.0 + nrm(ks[20], (DEPTH, D_INNER), 0.02),
        "w_m_out": nrm(ks[21], (DEPTH, D_INNER, D_MODEL), D_INNER ** -0.5),
        "w_o": nrm(ks[22], (DEPTH, D_MODEL, D_MODEL), D_MODEL ** -0.5),
        "norm2_w": 1.0 + nrm(ks[23], (DEPTH, D_MODEL), 0.02),
        "peer_w_q": nrm(ks[24], (DEPTH, D_MODEL, PEER_HEADS * D_KEY), D_MODEL ** -0.5),
        "peer_keys": nrm(ks[25], (DEPTH, PEER_HEADS, 2, N_KEYS, D_KEY // 2), (D_KEY // 2) ** -0.5),
        "peer_u": nrm(ks[26], (DEPTH, N_EXPERTS, D_MODEL), D_MODEL ** -0.5),
        "peer_v": nrm(ks[27], (DEPTH, N_EXPERTS, D_MODEL), (PEER_HEADS * PEER_TOPK) ** -0.5),
        "final_norm_w": 1.0 + nrm(ks[28], (D_MODEL,), 0.02),
    }


def reference(x_prompt, x_sample, cache_conf, cache_mconv, state_ssm, meta_tokens,
              norm1_w, w_in, b_in, conf_dw_w, conf_dw_b, conf_ln_g, conf_ln_b, w_conf_out, b_conf_out,
              m_conv_w, m_conv_b, dt_bias, A_log, D_skip, m_norm_w, w_m_out, w_o, norm2_w,
              peer_w_q, peer_keys, peer_u, peer_v, final_norm_w):
    bp, sp = x_prompt.shape[:2]
    xp = jnp.concatenate([jnp.broadcast_to(meta_tokens[None].astype(x_prompt.dtype), (bp, N_META, D_MODEL)),
                          x_prompt], axis=1)
    xs = x_sample
    prompt_segments = [(N_META, N_META), (sp, SSD_CHUNK)]
    sample_segments = [(xs.shape[1], xs.shape[1])]
    conf_p, mconv_p, ssm_p, conf_s, mconv_s, ssm_s = [], [], [], [], [], []
    for l in range(DEPTH):
        p = (norm1_w[l], w_in[l], b_in[l], conf_dw_w[l], conf_dw_b[l], conf_ln_g[l], conf_ln_b[l],
             w_conf_out[l], b_conf_out[l], m_conv_w[l], m_conv_b[l], dt_bias[l], A_log[l], D_skip[l],
             m_norm_w[l], w_m_out[l], w_o[l], norm2_w[l], peer_w_q[l], peer_keys[l], peer_u[l], peer_v[l])
        zc = jnp.zeros((bp, CONF_W - 1, C_CONV), xp.dtype)
        zm = jnp.zeros((bp, M_CONV_W - 1, CONV_DIM), xp.dtype)
        zh = jnp.zeros((bp, M_HEADS, M_HEADDIM, M_STATE), xp.dtype)
        xp, c1, m1, h1 = trunk_layer(xp, zc, zm, zh, prompt_segments, p)
        xs, c2, m2, h2 = trunk_layer(xs, cache_conf[l], cache_mconv[l], state_ssm[l], sample_segments, p)
        conf_p.append(c1); mconv_p.append(m1); ssm_p.append(h1)
        conf_s.append(c2); mconv_s.append(m2); ssm_s.append(h2)
    y_prompt = rmsnorm(xp, final_norm_w)[:, N_META:]
    y_sample = rmsnorm(xs, final_norm_w)
    return (y_prompt, y_sample,
            jnp.stack(conf_p), jnp.stack(mconv_p), jnp.stack(ssm_p),
            jnp.stack(conf_s), jnp.stack(mconv_s), jnp.stack(ssm_s))
```

```python
import numpy as np
from contextlib import ExitStack
import concourse.bass as bass
import concourse.mybir as mybir
from concourse.bass_utils import run_bass_kernel_spmd

F32 = mybir.dt.float32
I32 = mybir.dt.int32
U32 = mybir.dt.uint32
AF = mybir.ActivationFunctionType
ALU = mybir.AluOpType
AX = mybir.AxisListType

ENGS = ("pe", "dve", "act", "pool", "sp")


class Sched:
    def __init__(self, nc):
        self.nc = nc
        self.ops = []

    def c(self, eng, fn, r=(), w=(), quiet=False):
        self.ops.append(("q" if quiet else "c", eng, fn, tuple(r), tuple(w), None))

    def d(self, eng, fn, r=(), w=(), key=None):
        self.ops.append(("d", eng, fn, tuple(r), tuple(w), key))

    def build(self):
        nc = self.nc
        cnt = {e: 0 for e in ENGS}
        dcnt = {}
        lastw = {}
        readers = {}
        seen = {e: {} for e in ENGS}
        streams = {e: [] for e in ENGS}
        for kind, eng, fn, r, w, key in self.ops:
            deps = {}

            def add(tok):
                if tok is None:
                    return
                s, v = tok[0], tok[1]
                if len(tok) > 2 and s == "E:" + eng:
                    return
                if deps.get(s, 0) < v:
                    deps[s] = v

            for b in r:
                add(lastw.get(b))
            for b in w:
                add(lastw.get(b))
                for t in readers.get(b, ()):
                    add(t)
            if kind == "d":
                sname = "D:" + key
                if dcnt.get(key, 0) > 0:
                    add((sname, 16 * dcnt[key]))
                dcnt[key] = dcnt.get(key, 0) + 1
                tok = (sname, 16 * dcnt[key])
            elif kind == "q":
                tok = ("E:" + eng, cnt[eng] + 1, "q")
            else:
                cnt[eng] += 1
                tok = ("E:" + eng, cnt[eng])
            waits = []
            for s, v in deps.items():
                if seen[eng].get(s, 0) < v:
                    seen[eng][s] = v
                    waits.append((s, v))
            streams[eng].append((waits, fn, None if kind == "q" else tok))
            for b in r:
                readers.setdefault(b, []).append(tok)
            for b in w:
                lastw[b] = tok
                readers[b] = []
        final_waits = [("D:" + k, 16 * n) for k, n in dcnt.items()]
        final_eng = [("E:" + e, cnt[e]) for e in ENGS if cnt[e] > 0]
        semnames = ["E:" + e for e in ENGS if cnt[e] > 0] + ["D:" + k for k in dcnt]
        self.n_sems = len(semnames)
        self.counts = dict(cnt)
        with ExitStack() as es:
            sems = {}
            for i, s in enumerate(semnames):
                sems[s] = es.enter_context(nc.semaphore("s%d" % i))
            block = es.enter_context(nc.Block())

            def mk(ename):
                def body(e):
                    for waits, fn, tok in streams[ename]:
                        for s, v in waits:
                            e.wait_ge(sems[s], v)
                        ins = fn(e)
                        if tok is None:
                            continue
                        s, v = tok
                        ins.then_inc(sems[s], 16 if s.startswith("D:") else 1)
                    if ename == "sp":
                        for s, v in final_waits + final_eng:
                            e.wait_ge(sems[s], v)
                return body

            if streams["pe"]:
                block.tensor(mk("pe"))
            if streams["dve"]:
                block.vector(mk("dve"))
            if streams["act"]:
                block.scalar(mk("act"))
            if streams["pool"]:
                block.gpsimd(mk("pool"))
            block.sync(mk("sp"))


D = 1024
NIN = 10272
NCH = 504
C_BG, C_BX, C_CW, C_CB, C_LG, C_LB, C_MW, C_MB, C_N1, C_MN = 0, 16, 48, 296, 304, 312, 320, 448, 480, 488
K_ID, K_UP, K_GP, K_US, K_GS, K_ONE, K_ONEC, K_SI, K_EPS, K_IOTA, K_NCONST = 0, 128, 256, 384, 512, 640, 768, 896, 904, 912, 1168
NS_RING = 3


def _consts():
    c = np.zeros((128, K_NCONST), np.float32)
    p = np.arange(128)
    c[:, K_ID:K_ID + 128] = (p[:, None] == p[None, :])
    c[:, K_UP:K_UP + 128] = (p[:, None] <= p[None, :])
    c[:, K_GP:K_GP + 128] = (p[:, None] > p[None, :])
    same = (p[:, None] // 8) == (p[None, :] // 8)
    c[:, K_US:K_US + 128] = (p[:, None] <= p[None, :]) & same
    c[:, K_GS:K_GS + 128] = (p[:, None] > p[None, :]) & same
    c[:, K_ONE:K_ONE + 128] = 1.0
    c[:, K_ONEC:K_ONEC + 128] = 1.0 / 1024.0
    c[:, K_SI:K_SI + 8] = (p[:, None] // 8) == np.arange(8)[None, :]
    c[:, K_EPS] = 1e-6
    c[:, K_EPS + 1] = 1.0
    c[:, K_EPS + 2] = 0.0
    c[:, K_IOTA:K_IOTA + 256] = np.arange(256, dtype=np.float32)[None, :]
    return c


def build_program(n_ptiles=17, do_sample=True):
    nc = bass.Bass("TRN2", target_bir_lowering=False)
    din = lambda n, s, dt=F32: nc.dram_tensor(n, s, dt, kind="ExternalInput").ap()
    dout = lambda n, s, dt=F32: nc.dram_tensor(n, s, dt, kind="ExternalOutput").ap()
    xp_d = din("xp", [2048, D]); meta_d = din("meta", [16, D]); xs_d = din("xs", [128, D])
    cconf_d = din("cconf", [2, 16, 30, 1024]); cmconv_d = din("cmconv", [2, 16, 3, 4096])
    sssm_d = din("sssm", [2, 16, 2048, 128])
    w_in_d = din("w_in", [2, D, NIN]); b_in_d = din("b_in", [2, NIN])
    w_co_d = din("w_conf_out", [2, D, D]); b_co_d = din("b_conf_out", [2, D])
    w_mo_d = din("w_m_out", [2, 2048, D]); w_o_d = din("w_o", [2, D, D])
    w_q_d = din("peer_w_q", [2, D, 2048]); keysT_d = din("keysT", [2, 128, 16, 128])
    U_d = [din("peer_u0", [16384, D]), din("peer_u1", [16384, D])]; V_d = [din("peer_v0", [16384, D]), din("peer_v1", [16384, D])]
    chp_d = din("chp", [2, 128, NCH]); consts_d = din("consts", [128, K_NCONST])
    n2w_d = din("norm2_w", [2, D]); fnw_d = din("final_norm_w", [1, D])
    alog_d = din("A_log", [2, 32]); dsk_d = din("D_skip", [2, 32]); dtb_d = din("dt_bias", [2, 32])
    yp_d = dout("y_prompt", [2048, D]); ys_d = dout("y_sample", [128, D])
    confp_d = dout("conf_p", [2, 30, 1024]); mconvp_d = dout("mconv_p", [2, 3, 4096])
    ssmp_d = dout("ssm_p", [2, 2048, 128])
    confs_d = dout("conf_s", [2, 16, 30, 1024]); mconvs_d = dout("mconv_s", [2, 16, 3, 4096])
    ssms_d = dout("ssm_s", [2, 16, 2048, 128])

    es = ExitStack()
    sb = lambda n, s, dt=F32: es.enter_context(nc.sbuf_tensor(n, s, dt))
    XB = [sb("x0", [128, D]), sb("x1", [128, D])]
    XN2S = [sb("xn2_0", [128, D]), sb("xn2_1", [128, D])]
    GSMS = [sb("gsm0", [128, 128]), sb("gsm1", [128, 128])]
    ACC = sb("acc", [128, D])
    RING = sb("ring", [128, NS_RING, D])
    Hs = [sb("h0", [128, 16, 128]), sb("h1", [128, 16, 128])]
    histA = [sb("histA0", [128, 8, 30]), sb("histA1", [128, 8, 30])]
    histM = [sb("histM0", [128, 32, 3]), sb("histM1", [128, 32, 3])]
    CST = sb("consts_sb", [128, K_NCONST])
    CHP = [sb("chp0", [128, NCH]), sb("chp1", [128, NCH])]
    AB = [sb("ab0", [128, 32]), sb("ab1", [128, 32])]
    DK = [sb("dk0", [128, 32]), sb("dk1", [128, 32])]
    WB = [sb("wb%d" % i, [128, 9, 256]) for i in range(3)]
    DTB = sb("dtb", [1, 64])
    SM = sb("small", [128, 512])
    DECS = sb("decS", [128, 16, 8])
    IDSS = [sb("ids_i0", [128, 128], I32), sb("ids_i1", [128, 128], I32)]
    PR = sb("peer_small", [128, 3, 128])
    RT = sb("route_small", [128, 176])
    CM = sb("cm", [128, 6656])
    SL = {n: sb("slot" + n, [128, 2048]) for n in ["A", "X", "Bc", "R", "E", "Y", "T", "H", "Z", "G", "C"]}
    ps = lambda n, s: es.enter_context(nc.psum_tensor(n, s, F32))
    PA = ps("pA", [128, 1024]); PB = ps("pB", [128, 1024]); PY = ps("pY", [128, 2048])
    banks = [(PA, 0, "pA0"), (PA, 1, "pA1"), (PB, 0, "pB0"), (PB, 1, "pB1")]

    def bank(i):
        t, k, n = banks[i]
        return t[:, k * 512:(k + 1) * 512], n

    ybank = lambda i: (PY[:, i * 512:(i + 1) * 512], "pY%d" % i)
    YN = ["pY0", "pY1", "pY2", "pY3"]

    S = Sched(nc)
    YALL = ["Y", "Y2", "Y3", "Y4", "Y5_0", "Y5_1", "Y5_2"] + ["rt_%sb" % k_ for k_ in ("tv", "ti", "tf", "i1", "sc", "fi", "ff", "nm", "es")]
    cst = lambda col, n=128: CST[:, col:col + n]
    ident = cst(K_ID)
    EPS = CST[:, K_EPS:K_EPS + 1]
    ONE1 = CST[:, K_EPS + 1:K_EPS + 2]
    ZERO1 = CST[:, K_EPS + 2:K_EPS + 3]

    S.d("sp", lambda e: e.dma_start(out=CST[:], in_=consts_d), w=["CST"], key="cst")
    for l in range(2):
        S.d("sp", lambda e, l=l: e.dma_start(out=CHP[l][:], in_=chp_d[l]), w=["chp%d" % l], key="chp%d" % l)
        S.d("sp", lambda e, l=l: e.dma_start(out=AB[l][:], in_=alog_d[l:l + 1, :].broadcast_to([128, 32])), w=["ab%d" % l], key="ab%d" % l)
        S.d("sp", lambda e, l=l: e.dma_start(out=DK[l][:], in_=dsk_d[l:l + 1, :].broadcast_to([128, 32])), w=["dk%d" % l], key="dk%d" % l)
        S.c("act", lambda e, l=l: e.activation(out=AB[l][:], in_=AB[l][:], func=AF.Exp), r=["ab%d" % l], w=["ab%d" % l])
        S.c("dve", lambda e, l=l: e.tensor_scalar(out=AB[l][:], in0=AB[l][:], scalar1=-1.0, scalar2=None, op0=ALU.mult), r=["ab%d" % l], w=["ab%d" % l])
        S.c("pool", lambda e, l=l: e.memset(Hs[l][:], 0.0), w=["h%d" % l])
        S.c("pool", lambda e, l=l: e.memset(histA[l][:], 0.0), w=["histA%d" % l])
        S.c("pool", lambda e, l=l: e.memset(histM[l][:], 0.0), w=["histM%d" % l])

    wb_i = [0]

    def wload(src_ap, ncols, bias_ap=None):
        i = wb_i[0] % 3
        wb_i[0] += 1
        wn = "wb%d" % i
        S.d("sp", lambda e: e.dma_start(out=WB[i][:, 0:8, 0:ncols], in_=src_ap.rearrange("(k p) c -> p k c", p=128)), w=[wn], key=wn)
        if bias_ap is not None:
            S.d("sp", lambda e: e.dma_start(out=WB[i][0:1, 8, 0:ncols], in_=bias_ap), w=[wn + "b"], key=wn + "b")
        return WB[i], wn

    bk_i = [0]
    bank_pool = [0, 1, 2, 3]

    def nextbank():
        b = bank(bank_pool[bk_i[0] % len(bank_pool)])
        bk_i[0] += 1
        return b

    def transpose_into(dst_fn, src_fn, n_items, np_in, nf_in, dst_names, src_names, evac=None):
        per = 4
        for b0 in range(0, n_items, per):
            bt, bn = nextbank()
            n = min(per, n_items - b0)
            for q in range(n):
                j = b0 + q
                S.c("pe", lambda e, j=j, q=q, bt=bt: e.transpose(bt[:nf_in, q * 128:q * 128 + np_in], src_fn(j), ident[:np_in, :np_in]),
                    r=list(src_names) + ["CST"], w=[bn], quiet=(q < n - 1))
            dst_fn(b0, n, bt, bn)

    def layer_step(l, T):
        nt, nseq, L = T["nt"], T["nseq"], T["L"]
        samp = T["kind"] == "S"
        Umask = cst(K_US) if samp else cst(K_UP)
        Gseq = cst(K_GS) if samp else cst(K_GP)
        Gmask = cst(K_GP)
        chp = CHP[l]
        cn = "chp%d" % l
        cat = CM[:, 0:8 * nseq * (30 + L)].rearrange("p (j b t) -> p j b t", j=8, b=nseq)
        mcat = CM[:, 2464:2464 + 32 * nseq * (3 + L)].rearrange("p (j b t) -> p j b t", j=32, b=nseq)
        A_, X_, Bc_, R_, E_, Y_, T_, H_, Z_, G_, C_ = [SL[n] for n in ["A", "X", "Bc", "R", "E", "Y", "T", "H", "Z", "G", "C"]]
        X = T["X"]
        xname = T["xname"]
        xn = A_[:, 0:1024]
        xnT = A_[:, 1024:2048].rearrange("p (k t) -> p k t", k=8)
        v3 = lambda ap, b: ap.rearrange("p (b l) -> p b l", b=b)
        catn = lambda j: "cat%d" % j
        mcatn = lambda j: "mcat%d" % j
        ALLCAT = [catn(j) for j in range(8)]
        ALLMCAT = [mcatn(j) for j in range(32)]
        early = not (samp or T.get("last"))
        cacc = R_[:, 0:1024].rearrange("p (j t) -> p j t", j=8)
        csq = R_[:, 1024:2048].rearrange("p (j t) -> p j t", j=8)
        xcx = T_.rearrange("p (j t) -> p j t", j=16)
        xcb = C_.rearrange("p (j t) -> p j t", j=16)

        def conv_a_tile(j):
            acc = v3(cacc[:, j, :nt], nseq)
            S.c("dve", lambda e: e.tensor_scalar(out=acc, in0=cat[:, j, :, 0:L], scalar1=chp[:, C_CW + j * 31:C_CW + j * 31 + 1], scalar2=chp[:, C_CB + j:C_CB + j + 1], op0=ALU.mult, op1=ALU.add), r=[catn(j), cn], w=["R"])
            for k in range(1, 31):
                S.c("dve", lambda e, k=k: e.scalar_tensor_tensor(out=acc, in0=cat[:, j, :, k:k + L], scalar=chp[:, C_CW + j * 31 + k:C_CW + j * 31 + k + 1], in1=acc, op0=ALU.mult, op1=ALU.add), r=[catn(j), cn, "R"], w=["R"])

        def conv_b_tile(j):
            dst = (xcx if j < 16 else xcb)
            sn = "T" if j < 16 else "C"
            acc = v3(dst[:, j % 16, :nt], nseq)
            S.c("dve", lambda e: e.tensor_scalar(out=acc, in0=mcat[:, j, :, 0:L], scalar1=chp[:, C_MW + j * 4:C_MW + j * 4 + 1], scalar2=chp[:, C_MB + j:C_MB + j + 1], op0=ALU.mult, op1=ALU.add), r=[mcatn(j), cn], w=[sn])
            for k in range(1, 4):
                S.c("dve", lambda e, k=k: e.scalar_tensor_tensor(out=acc, in0=mcat[:, j, :, k:k + L], scalar=chp[:, C_MW + j * 4 + k:C_MW + j * 4 + k + 1], in1=acc, op0=ALU.mult, op1=ALU.add), r=[mcatn(j), cn, sn], w=[sn])

        if samp:
            s0 = T["s0"]
            for q4 in range(nseq // 4):
                S.d("sp", lambda e, q4=q4: e.dma_start(out=Y_[0:120, 0:1024], in_=cconf_d[l, s0 + 4 * q4:s0 + 4 * q4 + 4].rearrange("b r c -> (b r) c")), w=YALL, key="ldY")
                for j0 in range(0, 8, 4):
                    bt, bn = nextbank()
                    for q in range(4):
                        j = j0 + q
                        S.c("pe", lambda e, j=j, q=q, bt=bt: e.transpose(bt[:, q * 128:q * 128 + 120], Y_[0:120, j * 128:(j + 1) * 128], ident[:120, :120]), r=["Y", "CST"], w=[bn], quiet=(q < 3))
                    S.c("act", lambda e, j0=j0, bt=bt, q4=q4: e.activation(out=cat[:, j0:j0 + 4, 4 * q4:4 * q4 + 4, 0:30],
                        in_=bt[:, :].rearrange("p (q c) -> p q c", q=4)[:, :, 0:120].rearrange("p q (b r) -> p q b r", b=4), func=AF.Identity), r=[bn], w=[catn(j0 + q_) for q_ in range(4)])
            S.d("sp", lambda e: e.dma_start(out=Y_[0:3 * nseq, 0:2048], in_=cmconv_d[l, s0:s0 + nseq, :, 0:2048].rearrange("b r c -> (b r) c")), w=YALL, key="ldY")
            S.d("sp", lambda e: e.dma_start(out=T_[0:3 * nseq, 0:2048], in_=cmconv_d[l, s0:s0 + nseq, :, 2048:4096].rearrange("b r c -> (b r) c")), w=["T"], key="ldT")
            nr = 3 * nseq
            for half, src, sname in ((0, Y_, "Y"), (1, T_, "T")):
                for j0 in range(0, 16, 4):
                    bt, bn = nextbank()
                    for q in range(4):
                        j = j0 + q
                        S.c("pe", lambda e, j=j, q=q, bt=bt, src=src: e.transpose(bt[:, q * 128:q * 128 + nr], src[0:nr, j * 128:(j + 1) * 128], ident[:nr, :nr]), r=[sname, "CST"], w=[bn], quiet=(q < 3))
                    jj = half * 16 + j0
                    S.c("act", lambda e, jj=jj, bt=bt: e.activation(out=mcat[:, jj:jj + 4, :, 0:3],
                        in_=bt[:, :].rearrange("p (q c) -> p q c", q=4)[:, :, 0:nr].rearrange("p q (b r) -> p q b r", b=nseq), func=AF.Identity), r=[bn], w=[mcatn(jj + q_) for q_ in range(4)])
        else:
            S.c("pool", lambda e: e.tensor_copy(cat[:, :, 0, 0:30], histA[l][:]), r=["histA%d" % l], w=ALLCAT)
            S.c("pool", lambda e: e.tensor_copy(mcat[:, :, 0, 0:3], histM[l][:]), r=["histM%d" % l], w=ALLMCAT)

        ss = SM[:, 0:1]
        S.c("dve", lambda e: e.memset(ss[:nt], 0.0), w=["sm_ss"])
        S.c("dve", lambda e: e.scalar_tensor_tensor(out=xn[:nt], in0=X[:nt], scalar=1.0, in1=X[:nt], op0=ALU.mult, op1=ALU.mult, accum_out=ss[:nt]), r=[xname], w=["A", "sm_ss"])
        S.c("act", lambda e: e.activation(out=ss[:nt], in_=ss[:nt], func=AF.Sqrt, bias=EPS[:nt], scale=1.0 / 1024.0), r=["sm_ss", "CST"], w=["sm_ss"])
        S.c("dve", lambda e: e.reciprocal(out=ss[:nt], in_=ss[:nt]), r=["sm_ss"], w=["sm_ss"])
        S.c("dve", lambda e: e.tensor_scalar(out=xn[:nt], in0=X[:nt], scalar1=ss[:nt], scalar2=None, op0=ALU.mult), r=[xname, "sm_ss"], w=["A"])
        for k0 in range(0, 8, 4):
            bt, bn = nextbank()
            for q in range(4):
                k = k0 + q
                S.c("pe", lambda e, k=k, q=q, bt=bt: e.transpose(bt[:, q * 128:q * 128 + nt], xn[:nt, k * 128:(k + 1) * 128], ident[:nt, :nt]), r=["A", "CST"], w=[bn], quiet=(q < 3))
            for q in range(4):
                k = k0 + q
                S.c("act", lambda e, k=k, q=q, bt=bt: e.activation(out=xnT[:, k, :nt], in_=bt[:, q * 128:q * 128 + nt], func=AF.Identity, scale=chp[:, C_N1 + k:C_N1 + k + 1]), r=[bn, cn], w=["A2"])

        BW = 256
        zs = Z_
        gates = G_
        dtr = SM[:, 64:96]
        sgt = [SM[:, 256:384], SM[:, 384:512]]
        segs = [("glu", 0, 2048), ("z", 2048, 4096), ("xbc", 4096, 8192), ("dt", 8192, 8224), ("gate", 8224, 10272)]
        for sname, cs0, cs1 in segs:
            for c0 in range(cs0, cs1, BW):
                ncols = min(BW, cs1 - c0)
                chan_major = sname in ("glu", "xbc")
                wbt, wn = wload(w_in_d[l, :, c0:c0 + ncols], ncols, None if chan_major else b_in_d[l:l + 1, c0:c0 + ncols])
                if chan_major:
                    for f in range(ncols // 128):
                        bt, bn = nextbank()
                        for k in range(8):
                            S.c("pe", lambda e, f=f, k=k, bt=bt, wbt=wbt: e.matmul(bt[:, :nt], wbt[:, k, f * 128:(f + 1) * 128], xnT[:, k, :nt], start=(k == 0), stop=(k == 7)), r=[wn, "A2"], w=[bn], quiet=(k < 7))
                        fg = (c0 - cs0) // 128 + f
                        if sname == "glu":
                            if fg < 8:
                                S.c("act", lambda e, fg=fg, bt=bt: e.activation(out=cat[:, fg, :, 30:30 + L], in_=v3(bt[:, :nt], nseq), func=AF.Identity, bias=chp[:, C_BG + fg:C_BG + fg + 1]), r=[bn, cn], w=[catn(fg)])
                            else:
                                j = fg - 8
                                sg = sgt[j % 2]
                                sgn = "sg%d" % (j % 2)
                                S.c("act", lambda e, fg=fg, bt=bt, sg=sg: e.activation(out=sg[:, :nt], in_=bt[:, :nt], func=AF.Sigmoid, bias=chp[:, C_BG + fg:C_BG + fg + 1]), r=[bn, cn], w=[sgn])
                                S.c("dve", lambda e, j=j, sg=sg: e.tensor_tensor(out=cat[:, j, :, 30:30 + L], in0=cat[:, j, :, 30:30 + L], in1=v3(sg[:, :nt], nseq), op=ALU.mult), r=[catn(j), sgn], w=[catn(j)])
                                if early:
                                    conv_a_tile(j)
                        else:
                            j = fg
                            S.c("act", lambda e, j=j, bt=bt: e.activation(out=mcat[:, j, :, 3:3 + L], in_=v3(bt[:, :nt], nseq), func=AF.Identity, bias=chp[:, C_BX + j:C_BX + j + 1]), r=[bn, cn], w=[mcatn(j)])
                            if early:
                                conv_b_tile(j)
                else:
                    bt, bn = nextbank()
                    for k in range(8):
                        S.c("pe", lambda e, k=k, bt=bt, wbt=wbt, ncols=ncols: e.matmul(bt[:nt, :ncols], xnT[:, k, :nt], wbt[:, k, 0:ncols], start=(k == 0), stop=False), r=[wn, "A2"], w=[bn], quiet=True)
                    if sname == "dt":
                        S.d("sp", lambda e: e.dma_start(out=DTB[0:1, 0:32], in_=dtb_d[l:l + 1, :]), w=["dtb"], key="dtb")
                        S.c("pe", lambda e, bt=bt: e.matmul(bt[:nt, :32], cst(K_ONE)[0:1, :nt], DTB[0:1, 0:32], start=False, stop=False), r=["dtb", "CST"], w=[bn])
                    S.c("pe", lambda e, bt=bt, wbt=wbt, ncols=ncols: e.matmul(bt[:nt, :ncols], cst(K_ONE)[0:1, :nt], wbt[0:1, 8, 0:ncols], start=False, stop=True), r=[wn + "b", "CST"], w=[bn])
                    cc = c0 - cs0
                    if sname == "z":
                        S.c("act", lambda e, cc=cc, bt=bt, ncols=ncols: e.activation(out=zs[:nt, cc:cc + ncols], in_=bt[:nt, :ncols], func=AF.Silu), r=[bn], w=["Z"])
                    elif sname == "dt":
                        S.c("act", lambda e, bt=bt: e.activation(out=dtr[:nt], in_=bt[:nt, 0:32], func=AF.Exp), r=[bn], w=["sm_dt"])
                        S.c("act", lambda e: e.activation(out=dtr[:nt], in_=dtr[:nt], func=AF.Ln, bias=ONE1[:nt]), r=["sm_dt", "CST"], w=["sm_dt"])
                    else:
                        S.c("act", lambda e, cc=cc, bt=bt, ncols=ncols: e.activation(out=gates[:nt, cc:cc + ncols], in_=bt[:nt, :ncols], func=AF.Sigmoid), r=[bn], w=["G"])

        if samp or T.get("last"):
            utm = Y_[:, 0:1024]
            def evac_u(b0, n, bt, bn):
                S.c("act", lambda e: e.activation(out=utm[:nt, b0 * 128:(b0 + n) * 128], in_=bt[:nt, 0:n * 128], func=AF.Identity), r=[bn], w=["Y"])
            ust = E_[:, 0:1024].rearrange("p (j t) -> p j t", j=8)
            S.c("pool", lambda e: e.tensor_copy(ust[:, :, :nt].rearrange("p j (b t) -> p j b t", b=nseq), cat[:, :, :, 30:30 + L]), r=ALLCAT, w=["E"])
            transpose_into(evac_u, lambda j: ust[:, j, :nt], 8, 128, nt, ["Y"], ["E"])
            if samp:
                s0 = T["s0"]
                for b in range(nseq):
                    S.d("sp", lambda e, b=b: e.dma_start(out=confs_d[l, s0 + b, 22:30, :], in_=utm[b * 8:(b + 1) * 8, :]), r=["Y"], key="oconf")
                S.d("sp", lambda e: e.dma_start(out=confs_d[l, s0:s0 + nseq, 0:22, :], in_=cconf_d[l, s0:s0 + nseq, 8:30, :]), key="oconf2")
            else:
                S.d("sp", lambda e: e.dma_start(out=confp_d[l], in_=utm[nt - 30:nt, :]), r=["Y"], key="oconf")
            xst = T_.rearrange("p (j t) -> p j t", j=16)
            for half in range(2):
                S.c("pool", lambda e, half=half: e.tensor_copy(xst[:, :, :nt].rearrange("p j (b t) -> p j b t", b=nseq), mcat[:, half * 16:(half + 1) * 16, :, 3:3 + L]), r=[mcatn(half * 16 + t_) for t_ in range(16)], w=["T"])
                xtm = H_
                def evac_x(b0, n, bt, bn):
                    S.c("act", lambda e: e.activation(out=xtm[:nt, b0 * 128:(b0 + n) * 128], in_=bt[:nt, 0:n * 128], func=AF.Identity), r=[bn], w=["H"])
                transpose_into(evac_x, lambda j: xst[:, j, :nt], 16, 128, nt, ["H"], ["T"])
                if samp:
                    for b in range(nseq):
                        S.d("sp", lambda e, b=b, half=half: e.dma_start(out=mconvs_d[l, s0 + b, :, half * 2048:(half + 1) * 2048], in_=xtm[b * 8 + 5:b * 8 + 8, :]), r=["H"], key="omconv")
                else:
                    S.d("sp", lambda e, half=half: e.dma_start(out=mconvp_d[l, :, half * 2048:(half + 1) * 2048], in_=xtm[nt - 3:nt, :]), r=["H"], key="omconv")
        if not samp:
            S.c("pool", lambda e: e.tensor_copy(histA[l][:], cat[:, :, 0, L:L + 30]), r=ALLCAT, w=["histA%d" % l])

        mark_a = len(S.ops)
        bank_pool[:] = [0, 1]
        if not early:
            for j in range(8):
                conv_a_tile(j)
        S.c("act", lambda e: e.activation(out=csq[:, :, :nt], in_=cacc[:, :, :nt], func=AF.Square), r=["R"], w=["R2"])
        bm, bmn = nextbank()
        bq, bqn = nextbank()
        for j in range(8):
            S.c("pe", lambda e, j=j: e.matmul(bm[:, :nt], cst(K_ONEC), cacc[:, j, :nt], start=(j == 0), stop=(j == 7)), r=["R", "CST"], w=[bmn], quiet=(j < 7))
        for j in range(8):
            S.c("pe", lambda e, j=j: e.matmul(bq[:, :nt], cst(K_ONEC), csq[:, j, :nt], start=(j == 0), stop=(j == 7)), r=["R2", "CST"], w=[bqn], quiet=(j < 7))
        cs = E_[:, 0:1024].rearrange("p (j t) -> p j t", j=8)
        mean = E_[:, 1024:1152]
        var = E_[:, 1152:1280]
        S.c("act", lambda e: e.activation(out=mean[:, :nt], in_=bm[:, :nt], func=AF.Identity), r=[bmn], w=["E2"])
        S.c("dve", lambda e: e.tensor_tensor(out=var[:, :nt], in0=mean[:, :nt], in1=mean[:, :nt], op=ALU.mult), r=["E2"], w=["E2"])
        S.c("dve", lambda e: e.tensor_tensor(out=var[:, :nt], in0=bq[:, :nt], in1=var[:, :nt], op=ALU.subtract), r=[bqn, "E2"], w=["E2"])
        S.c("act", lambda e: e.activation(out=var[:, :nt], in_=var[:, :nt], func=AF.Sqrt, bias=EPS), r=["E2", "CST"], w=["E2"])
        S.c("dve", lambda e: e.reciprocal(out=var[:, :nt], in_=var[:, :nt]), r=["E2"], w=["E2"])
        S.c("dve", lambda e: e.tensor_tensor(out=cs[:, :, :nt], in0=cacc[:, :, :nt], in1=mean[:, :nt].unsqueeze(1).broadcast_to([128, 8, nt]), op=ALU.subtract), r=["R", "E2"], w=["E"])
        S.c("dve", lambda e: e.tensor_tensor(out=cs[:, :, :nt], in0=cs[:, :, :nt], in1=var[:, :nt].unsqueeze(1).broadcast_to([128, 8, nt]), op=ALU.mult), r=["E", "E2"], w=["E"])
        for j in range(8):
            S.c("act", lambda e, j=j: e.activation(out=cs[:, j, :nt], in_=cs[:, j, :nt], func=AF.Silu, bias=chp[:, C_LB + j:C_LB + j + 1], scale=chp[:, C_LG + j:C_LG + j + 1]), r=["E", cn], w=["E"])
        for cb in range(4):
            wbt, wn = wload(w_co_d[l, :, cb * 256:(cb + 1) * 256], 256, b_co_d[l:l + 1, cb * 256:(cb + 1) * 256])
            bt, bn = nextbank()
            for k in range(8):
                S.c("pe", lambda e, k=k, bt=bt, wbt=wbt: e.matmul(bt[:nt, 0:256], cs[:, k, :nt], wbt[:, k, :], start=(k == 0), stop=False), r=[wn, "E"], w=[bn], quiet=True)
            S.c("pe", lambda e, bt=bt, wbt=wbt: e.matmul(bt[:nt, 0:256], cst(K_ONE)[0:1, :nt], wbt[0:1, 8, :], start=False, stop=True), r=[wn + "b", "CST"], w=[bn])
            S.c("dve", lambda e, cb=cb, bt=bt: e.tensor_tensor(out=gates[:nt, cb * 256:(cb + 1) * 256], in0=bt[:nt, 0:256], in1=gates[:nt, cb * 256:(cb + 1) * 256], op=ALU.mult), r=[bn, "G"], w=["G"])

        mark_b = len(S.ops)
        bank_pool[:] = [2, 3]
        if not early:
            for j in range(32):
                conv_b_tile(j)
        if not samp:
            S.c("pool", lambda e: e.tensor_copy(histM[l][:], mcat[:, :, 0, L:L + 3]), r=ALLMCAT, w=["histM%d" % l])
        S.c("act", lambda e: e.activation(out=xcx[:, :, :nt], in_=xcx[:, :, :nt], func=AF.Silu), r=["T"], w=["T"])
        S.c("act", lambda e: e.activation(out=xcb[:, :, :nt], in_=xcb[:, :, :nt], func=AF.Silu), r=["C"], w=["C"])
        BT = lambda g: xcb[:, g, :nt]
        CT = lambda g: xcb[:, 8 + g, :nt]
        xs_tm = X_
        B_tm = Bc_[:, 0:1024]
        cbm = Bc_[:, 1024:2048].rearrange("p (g t) -> p g t", g=8)

        def evac_xs(b0, n, bt, bn):
            S.c("act", lambda e: e.activation(out=xs_tm[:nt, b0 * 128:(b0 + n) * 128], in_=bt[:nt, 0:n * 128], func=AF.Identity), r=[bn], w=["X"])
        transpose_into(evac_xs, lambda j: xcx[:, j, :nt], 16, 128, nt, ["X"], ["T"])

        def evac_b(b0, n, bt, bn):
            S.c("act", lambda e: e.activation(out=B_tm[:nt, b0 * 128:(b0 + n) * 128], in_=bt[:nt, 0:n * 128], func=AF.Identity), r=[bn], w=["Bc"])
        transpose_into(evac_b, lambda j: xcb[:, j, :nt], 8, 128, nt, ["Bc"], ["C"])

        a_ = SM[:, 96:128]
        ew = SM[:, 128:192]
        S.c("dve", lambda e: e.tensor_tensor(out=a_[:nt], in0=dtr[:nt], in1=AB[l][:nt], op=ALU.mult), r=["sm_dt", "ab%d" % l], w=["sm_a"])
        bt, bn = nextbank()
        S.c("pe", lambda e, bt=bt: e.matmul(bt[:nt, 0:32], Umask[:nt, :nt], a_[:nt, :], start=True, stop=True), r=["sm_a", "CST"], w=[bn], quiet=True)
        S.c("pe", lambda e, bt=bt: e.matmul(bt[:nt, 32:64], Gseq[:nt, :nt], a_[:nt, :], start=True, stop=True), r=["sm_a", "CST"], w=[bn])
        S.c("act", lambda e, bt=bt: e.activation(out=ew[:nt], in_=bt[:nt, 0:64], func=AF.Exp), r=[bn], w=["sm_ew"])
        eacs = ew[:, 0:32]
        wend = ew[:, 32:64]
        h3 = lambda ap: ap.rearrange("p (h q) -> p h q", h=32)
        bc3 = lambda ap: ap.unsqueeze(2).broadcast_to([nt, 32, 64])
        yacc = Y_
        tmp = T_
        S.c("dve", lambda e: e.tensor_tensor(out=h3(yacc[:nt]), in0=h3(xs_tm[:nt]), in1=bc3(DK[l][:nt]), op=ALU.mult), r=["X", "dk%d" % l], w=["Y"])
        S.c("dve", lambda e: e.tensor_tensor(out=h3(xs_tm[:nt]), in0=h3(xs_tm[:nt]), in1=bc3(dtr[:nt]), op=ALU.mult), r=["X", "sm_dt"], w=["X"])
        dx = xs_tm
        S.c("dve", lambda e: e.tensor_copy(h3(tmp[:nt]), bc3(a_[:nt])), r=["sm_a", "T"], w=["T"])
        bt, bn = nextbank()
        sicol = cst(K_SI, 8) if samp else cst(K_ONE, 1)
        for j in range(16):
            S.c("pe", lambda e, j=j, bt=bt: e.matmul(bt[:, j * nseq:(j + 1) * nseq], tmp[:nt, j * 128:(j + 1) * 128], sicol[:nt, 0:nseq], start=True, stop=True), r=["T", "CST"], w=[bn], quiet=(j < 15))
        S.c("act", lambda e, bt=bt: e.activation(out=DECS[:, :, 0:nseq], in_=bt[:, 0:16 * nseq].rearrange("p (j b) -> p j b", j=16), func=AF.Exp), r=[bn], w=["decs"])
        for g in range(8):
            bt2, bn2 = bank(2 + g // 4)
            S.c("pe", lambda e, g=g, bt2=bt2: e.matmul(bt2[:nt, (g % 4) * nt:(g % 4 + 1) * nt], BT(g), CT(g), start=True, stop=True), r=["C"], w=[bn2], quiet=(g % 4 != 3))
        for hb in range(2):
            bt2, bn2 = bank(2 + hb)
            S.c("dve", lambda e, hb=hb, bt2=bt2: e.tensor_tensor(out=cbm[:nt, hb * 4:(hb + 1) * 4, :nt], in0=bt2[:nt, 0:4 * nt].rearrange("p (g t) -> p g t", g=4), in1=Umask[:nt, :nt].unsqueeze(1).broadcast_to([nt, 4, nt]), op=ALU.mult), r=[bn2, "CST"], w=["Bc2"])

        hT = H_

        def load_hT(hsrc, hname):
            for j0 in range(0, 16, 4):
                bt, bn = bank(2 + (j0 // 4) % 2)
                for q in range(4):
                    j = j0 + q
                    S.c("pe", lambda e, j=j, q=q, bt=bt: e.transpose(bt[:, q * 128:(q + 1) * 128], hsrc[:, j, :], ident), r=[hname, "CST"], w=[bn], quiet=(q < 3))
                S.c("act", lambda e, j0=j0, bt=bt: e.activation(out=hT[:, j0 * 128:(j0 + 4) * 128], in_=bt[:, :], func=AF.Identity), r=[bn], w=["H"])

        if not samp:
            load_hT(Hs[l], "h%d" % l)
            for g in range(8):
                yb, ybn = ybank(g // 2)
                S.c("pe", lambda e, g=g, yb=yb: e.matmul(yb[:nt, (g % 2) * 256:(g % 2 + 1) * 256], CT(g), hT[:, g * 256:(g + 1) * 256], start=True, stop=True), r=["C", "H"], w=[ybn], quiet=(g % 2 == 0))
        else:
            s0 = T["s0"]
            S.c("dve", lambda e: e.memset(tmp[:nt], 0.0), w=["T"])
            for b in range(nseq):
                hb_ = Hs[b % 2]
                hbn = "h%d" % (b % 2)
                S.d("sp", lambda e, b=b, hb_=hb_: e.dma_start(out=hb_[:], in_=sssm_d[l, s0 + b].rearrange("(j p) n -> p j n", p=128)), w=[hbn], key="ld" + hbn)
                load_hT(hb_, hbn)
                for g in range(8):
                    yb, ybn = ybank(g // 2)
                    S.c("pe", lambda e, g=g, yb=yb: e.matmul(yb[:nt, (g % 2) * 256:(g % 2 + 1) * 256], CT(g), hT[:, g * 256:(g + 1) * 256], start=True, stop=True), r=["C", "H"], w=[ybn], quiet=(g % 2 == 0))
                S.c("dve", lambda e, b=b: e.scalar_tensor_tensor(out=tmp[:nt], in0=PY[:nt], scalar=cst(K_SI, 8)[:nt, b:b + 1], in1=tmp[:nt], op0=ALU.mult, op1=ALU.add), r=YN + ["T", "CST"], w=["T"])
            S.c("dve", lambda e: e.tensor_tensor(out=h3(tmp[:nt]), in0=h3(tmp[:nt]), in1=bc3(eacs[:nt]), op=ALU.mult), r=["sm_ew", "T"], w=["T"])
        if not samp:
            S.c("dve", lambda e: e.tensor_tensor(out=h3(tmp[:nt]), in0=h3(PY[:nt]), in1=bc3(eacs[:nt]), op=ALU.mult), r=YN + ["sm_ew", "T"], w=["T"])
        S.c("dve", lambda e: e.tensor_tensor(out=yacc[:nt], in0=yacc[:nt], in1=tmp[:nt], op=ALU.add), r=["Y", "T"], w=["Y"])

        a_ops = S.ops[mark_a:mark_b]
        b_ops = S.ops[mark_b:]
        del S.ops[mark_a:]
        S.ops.extend(merge(b_ops, a_ops) if len(a_ops) >= len(b_ops) else merge(a_ops, b_ops))
        bank_pool[:] = [0, 1, 2, 3]
        for hg in range(4):
            rhs_a = R_[:, (hg % 2) * 1024:(hg % 2 + 1) * 1024]
            rn = "R" if hg % 2 == 0 else "R2"
            dec = E_[:, (hg % 2) * 1024:(hg % 2 + 1) * 1024]
            en = "E" if hg % 2 == 0 else "E2"
            ra3 = rhs_a[:nt, 0:8 * nt].rearrange("p (h t) -> p h t", h=8)
            S.c("dve", lambda e, hg=hg, ra3=ra3: e.tensor_tensor(out=ra3, in0=a_[:nt, hg * 8:(hg + 1) * 8].unsqueeze(2).broadcast_to([nt, 8, nt]), in1=Umask[:nt, :nt].unsqueeze(1).broadcast_to([nt, 8, nt]), op=ALU.mult), r=["sm_a", "CST"], w=[rn])
            for i in range(2):
                bt, bn = bank(i)
                S.c("pe", lambda e, i=i, bt=bt, rhs_a=rhs_a: e.matmul(bt[:nt, 0:4 * nt], Gmask[:nt, :nt], rhs_a[:nt, i * 4 * nt:(i + 1) * 4 * nt], start=True, stop=True), r=[rn, "CST"], w=[bn])
                S.c("act", lambda e, i=i, bt=bt, dec=dec: e.activation(out=dec[:nt, i * 4 * nt:(i + 1) * 4 * nt], in_=bt[:nt, 0:4 * nt], func=AF.Exp), r=[bn], w=[en])
            d4 = dec[:nt, 0:8 * nt].rearrange("p (g r t) -> p g r t", g=2, r=4)
            S.c("dve", lambda e, hg=hg, d4=d4: e.tensor_tensor(out=d4, in0=d4, in1=cbm[:nt, 2 * hg:2 * hg + 2, :nt].unsqueeze(2).broadcast_to([nt, 2, 4, nt]), op=ALU.mult), r=[en, "Bc2"], w=[en])
            yb, ybn = ybank(hg)
            for hh in range(8):
                head = hg * 8 + hh
                S.c("pe", lambda e, hh=hh, head=head, yb=yb, dec=dec: e.matmul(yb[:nt, hh * 64:(hh + 1) * 64], dec[:nt, hh * nt:(hh + 1) * nt], dx[:nt, head * 64:(head + 1) * 64], start=True, stop=True), r=[en, "X"], w=[ybn], quiet=(hh < 7))
        S.c("dve", lambda e: e.tensor_tensor(out=yacc[:nt], in0=yacc[:nt], in1=PY[:nt], op=ALU.add), r=["Y"] + YN, w=["Y"])

        S.c("dve", lambda e: e.tensor_tensor(out=h3(dx[:nt]), in0=h3(dx[:nt]), in1=bc3(wend[:nt]), op=ALU.mult), r=["X", "sm_ew"], w=["X"])
        Xw = dx
        for b in range(nseq):
            if samp:
                s0 = T["s0"]
                Bm = R_[:, 1024:2048]
                S.c("dve", lambda e, b=b: e.tensor_scalar(out=Bm[:nt], in0=B_tm[:nt], scalar1=cst(K_SI, 8)[:nt, b:b + 1], scalar2=None, op0=ALU.mult), r=["Bc", "CST"], w=["R2"])
                bmn_ = "R2"
                hb_ = Hs[b % 2]
                hbn = "h%d" % (b % 2)
                S.d("sp", lambda e, b=b, hb_=hb_: e.dma_start(out=hb_[:], in_=sssm_d[l, s0 + b].rearrange("(j p) n -> p j n", p=128)), w=[hbn], key="ld" + hbn)
            else:
                Bm = B_tm
                bmn_ = "Bc"
                hb_ = Hs[l]
                hbn = "h%d" % l
            for j in range(16):
                yb, ybn = ybank(j // 4)
                S.c("pe", lambda e, j=j, yb=yb, Bm=Bm: e.matmul(yb[:, (j % 4) * 128:(j % 4 + 1) * 128], Xw[:nt, j * 128:(j + 1) * 128], Bm[:nt, (j // 2) * 128:(j // 2 + 1) * 128], start=True, stop=True), r=["X", bmn_], w=[ybn], quiet=(j % 4 != 3))
            S.c("dve", lambda e, b=b, hb_=hb_: e.tensor_tensor(out=hb_[:], in0=hb_[:], in1=DECS[:, :, b:b + 1].broadcast_to([128, 16, 128]), op=ALU.mult), r=[hbn, "decs"], w=[hbn])
            S.c("dve", lambda e, hb_=hb_: e.tensor_tensor(out=hb_[:], in0=hb_[:], in1=PY[:, :].rearrange("p (j n) -> p j n", j=16), op=ALU.add), r=[hbn] + YN, w=[hbn])
            if samp:
                S.d("sp", lambda e, b=b, hb_=hb_: e.dma_start(out=ssms_d[l, s0 + b].rearrange("(j p) n -> p j n", p=128), in_=hb_[:]), r=[hbn], key="ossm")
        if (not samp) and T.get("last"):
            S.d("sp", lambda e: e.dma_start(out=ssmp_d[l].rearrange("(j p) n -> p j n", p=128), in_=Hs[l][:]), r=["h%d" % l], key="ossm")

        ssq = SM[:, 192:200]
        g8 = lambda ap: ap.rearrange("p (g q) -> p g q", g=8)
        S.c("dve", lambda e: e.tensor_tensor(out=yacc[:nt], in0=yacc[:nt], in1=zs[:nt], op=ALU.mult), r=["Y", "Z"], w=["Y"])
        S.c("dve", lambda e: e.tensor_tensor(out=tmp[:nt], in0=yacc[:nt], in1=yacc[:nt], op=ALU.mult), r=["Y", "T"], w=["T"])
        S.c("dve", lambda e: e.tensor_reduce(out=ssq[:nt], in_=g8(tmp[:nt]), axis=AX.X, op=ALU.add), r=["T"], w=["sm_ssq"])
        S.c("act", lambda e: e.activation(out=ssq[:nt], in_=ssq[:nt], func=AF.Sqrt, bias=EPS[:nt], scale=1.0 / 256.0), r=["sm_ssq", "CST"], w=["sm_ssq"])
        S.c("dve", lambda e: e.reciprocal(out=ssq[:nt], in_=ssq[:nt]), r=["sm_ssq"], w=["sm_ssq"])
        S.c("dve", lambda e: e.tensor_tensor(out=g8(yacc[:nt]), in0=g8(yacc[:nt]), in1=ssq[:nt].unsqueeze(2).broadcast_to([nt, 8, 256]), op=ALU.mult), r=["Y", "sm_ssq"], w=["Y"])
        ynT = T_.rearrange("p (j t) -> p j t", j=16)
        for j0 in range(0, 16, 4):
            bt, bn = nextbank()
            for q in range(4):
                j = j0 + q
                S.c("pe", lambda e, j=j, q=q, bt=bt: e.transpose(bt[:, q * 128:q * 128 + nt], yacc[:nt, j * 128:(j + 1) * 128], ident[:nt, :nt]), r=["Y", "CST"], w=[bn], quiet=(q < 3))
            for q in range(4):
                j = j0 + q
                S.c("act", lambda e, j=j, q=q, bt=bt: e.activation(out=ynT[:, j, :nt], in_=bt[:, q * 128:q * 128 + nt], func=AF.Identity, scale=chp[:, C_MN + j:C_MN + j + 1]), r=[bn, cn], w=["T"])
        for cb in range(4):
            bt, bn = nextbank()
            for kh in range(2):
                wbt, wn = wload(w_mo_d[l, kh * 1024:(kh + 1) * 1024, cb * 256:(cb + 1) * 256], 256)
                for k in range(8):
                    S.c("pe", lambda e, k=k, kh=kh, bt=bt, wbt=wbt: e.matmul(bt[:nt, 0:256], ynT[:, kh * 8 + k, :nt], wbt[:, k, :], start=(kh == 0 and k == 0), stop=(kh == 1 and k == 7)), r=[wn, "T"], w=[bn], quiet=(k < 7))
            S.c("dve", lambda e, cb=cb, bt=bt: e.tensor_tensor(out=gates[:nt, 1024 + cb * 256:1024 + (cb + 1) * 256], in0=bt[:nt, 0:256], in1=gates[:nt, 1024 + cb * 256:1024 + (cb + 1) * 256], op=ALU.mult), r=[bn, "G"], w=["G"])
        S.c("dve", lambda e: e.tensor_tensor(out=gates[:nt, 0:1024], in0=gates[:nt, 0:1024], in1=gates[:nt, 1024:2048], op=ALU.add), r=["G"], w=["G"])
        mT = A_[:, 1024:2048].rearrange("p (k t) -> p k t", k=8)
        for k0 in range(0, 8, 4):
            bt, bn = nextbank()
            for q in range(4):
                k = k0 + q
                S.c("pe", lambda e, k=k, q=q, bt=bt: e.transpose(bt[:, q * 128:q * 128 + nt], gates[:nt, k * 128:(k + 1) * 128], ident[:nt, :nt]), r=["G", "CST"], w=[bn], quiet=(q < 3))
            S.c("act", lambda e, k0=k0, bt=bt: e.activation(out=mT[:, k0:k0 + 4, :nt], in_=bt[:, :].rearrange("p (q t) -> p q t", q=4)[:, :, :nt], func=AF.Identity), r=[bn], w=["A2"])
        for cb in range(4):
            wbt, wn = wload(w_o_d[l, :, cb * 256:(cb + 1) * 256], 256)
            bt, bn = nextbank()
            for k in range(8):
                S.c("pe", lambda e, k=k, bt=bt, wbt=wbt: e.matmul(bt[:nt, 0:256], mT[:, k, :nt], wbt[:, k, :], start=(k == 0), stop=(k == 7)), r=[wn, "A2"], w=[bn], quiet=(k < 7))
            S.c("dve", lambda e, cb=cb, bt=bt: e.tensor_tensor(out=X[:nt, cb * 256:(cb + 1) * 256], in0=X[:nt, cb * 256:(cb + 1) * 256], in1=bt[:nt, 0:256], op=ALU.add), r=[bn, xname], w=[xname])


    def peer_route(l, T, par):
        nt = T["nt"]
        X = T["X"]
        xname = T["xname"]
        A_, X_, Y_, T_, H_, G_ = [SL[n] for n in ["A", "X", "Y", "T", "H", "G"]]
        ss = SM[:, 0:1]
        hdot = PR[:, 0, :]; wts = PR[:, 1, :]; idsf = PR[:, 2, :]
        gsm = GSMS[par]; IDS = IDSS[par]; xn2 = XN2S[par]
        xn2n = "xn2_%d" % par; idn = "ids%d" % par; gn = "pr_g%d" % par
        xn2T = A_[:, 1024:2048].rearrange("p (k t) -> p k t", k=8)
        n2w = G_[:, 0:1024]
        S.d("sp", lambda e: e.dma_start(out=n2w, in_=n2w_d[l:l + 1, :].broadcast_to([128, D])), w=["G"], key="ldG")
        S.c("dve", lambda e: e.memset(ss[:nt], 0.0), w=["sm_ss"])
        S.c("dve", lambda e: e.scalar_tensor_tensor(out=xn2[:nt], in0=X[:nt], scalar=1.0, in1=X[:nt], op0=ALU.mult, op1=ALU.mult, accum_out=ss[:nt]), r=[xname], w=[xn2n, "sm_ss"])
        S.c("act", lambda e: e.activation(out=ss[:nt], in_=ss[:nt], func=AF.Sqrt, bias=EPS[:nt], scale=1.0 / 1024.0), r=["sm_ss", "CST"], w=["sm_ss"])
        S.c("dve", lambda e: e.reciprocal(out=ss[:nt], in_=ss[:nt]), r=["sm_ss"], w=["sm_ss"])
        S.c("dve", lambda e: e.scalar_tensor_tensor(out=xn2[:nt], in0=X[:nt], scalar=ss[:nt], in1=n2w[:nt], op0=ALU.mult, op1=ALU.mult), r=[xname, "sm_ss", "G"], w=[xn2n])
        for k0 in range(0, 8, 4):
            bt, bn = nextbank()
            for q in range(4):
                k = k0 + q
                S.c("pe", lambda e, k=k, q=q, bt=bt: e.transpose(bt[:, q * 128:q * 128 + nt], xn2[:nt, k * 128:(k + 1) * 128], ident[:nt, :nt]), r=[xn2n, "CST"], w=[bn], quiet=(q < 3))
            S.c("act", lambda e, k0=k0, bt=bt: e.activation(out=xn2T[:, k0:k0 + 4, :nt], in_=bt[:, :].rearrange("p (q t) -> p q t", q=4)[:, :, :nt], func=AF.Identity), r=[bn], w=["A2"])
        qT = X_.rearrange("p (c t) -> p c t", c=16)
        keysT = Y_.rearrange("p (c k) -> p c k", c=16)
        S.d("sp", lambda e: e.dma_start(out=keysT, in_=keysT_d[l]), w=YALL, key="ldY")
        for blk in range(8):
            wbt, wn = wload(w_q_d[l, :, blk * 256:(blk + 1) * 256], 256)
            for f in range(2):
                hc = blk * 2 + f
                bt, bn = nextbank()
                for k in range(8):
                    S.c("pe", lambda e, f=f, k=k, bt=bt, wbt=wbt: e.matmul(bt[:, :nt], wbt[:, k, f * 128:(f + 1) * 128], xn2T[:, k, :nt], start=(k == 0), stop=(k == 7)), r=[wn, "A2"], w=[bn], quiet=(k < 7))
                S.c("act", lambda e, hc=hc, bt=bt: e.activation(out=qT[:, hc, :nt], in_=bt[:, :nt], func=AF.Identity), r=[bn], w=["X"])
        sc_all = T_.rearrange("p (c k) -> p c k", c=16)
        for hc in range(16):
            yb, ybn = ybank(hc // 4)
            S.c("pe", lambda e, hc=hc, yb=yb: e.matmul(yb[:nt, (hc % 4) * 128:(hc % 4 + 1) * 128], qT[:, hc, :nt], keysT[:, hc, :], start=True, stop=True), r=["X", "Y"], w=[ybn], quiet=(hc % 4 != 3))
        S.c("act", lambda e: e.activation(out=T_[:nt, :], in_=PY[:nt, :], func=AF.Identity), r=YN, w=["T"])

        def mkset(base, rtb, junks):
            return dict(work=base[:, 0:128], cand=base[:, 256:512], candw=base[:, 512:768], idc=base[:, 768:1024], junk2=base[:, 1024:1280], junks=junks,
                        tv=rtb[:, 0:32], ti=rtb[:, 32:64], tiu=rtb[:, 64:96].bitcast(U32), scv=rtb[:, 96:112], esum=rtb[:, 112:113],
                        nmax=rtb[:, 113:114], i1s=rtb[:, 128:144], fiu=rtb[:, 144:160].bitcast(U32), ff=rtb[:, 160:176])

        def route_head(h, sc, sfx, hp):
            work, cand, candw, idc, junk2 = sc["work"], sc["cand"], sc["candw"], sc["idc"], sc["junk2"]
            tv, ti, tiu, scv, esum, nmax, i1s, fiu, ff = [sc[k_] for k_ in ("tv", "ti", "tiu", "scv", "esum", "nmax", "i1s", "fiu", "ff")]
            for c in range(2):
                src = sc_all[:nt, 2 * h + c, :]
                o = c * 16
                S.c("dve", lambda e, src=src, o=o: e.max(out=tv[:nt, o:o + 8], in_=src), r=["T"], w=["rt_tv" + sfx])
                S.c("dve", lambda e, src=src, o=o: e.max_index(out=tiu[:nt, o:o + 8], in_max=tv[:nt, o:o + 8], in_values=src), r=["T", "rt_tv" + sfx], w=["rt_ti" + sfx])
                S.c("dve", lambda e, src=src, o=o: e.match_replace(out=work[:nt], in_to_replace=tv[:nt, o:o + 8], in_values=src, imm_value=-1e30), r=["T", "rt_tv" + sfx], w=[hp])
                S.c("dve", lambda e, o=o: e.max(out=tv[:nt, o + 8:o + 16], in_=work[:nt]), r=[hp], w=["rt_tv" + sfx])
                S.c("dve", lambda e, o=o: e.max_index(out=tiu[:nt, o + 8:o + 16], in_max=tv[:nt, o + 8:o + 16], in_values=work[:nt]), r=[hp, "rt_tv" + sfx], w=["rt_ti" + sfx])
            S.c("dve", lambda e: e.tensor_copy(ti[:nt], tiu[:nt]), r=["rt_ti" + sfx], w=["rt_tf" + sfx])
            S.c("dve", lambda e: e.tensor_scalar(out=i1s[:nt], in0=ti[:nt, 0:16], scalar1=128.0, scalar2=None, op0=ALU.mult), r=["rt_tf" + sfx], w=["rt_i1" + sfx])
            c3 = lambda ap: ap.rearrange("p (a b) -> p a b", a=16)
            S.c("dve", lambda e: e.tensor_tensor(out=c3(cand[:nt]), in0=tv[:nt, 0:16].unsqueeze(2).broadcast_to([nt, 16, 16]), in1=tv[:nt, 16:32].unsqueeze(1).broadcast_to([nt, 16, 16]), op=ALU.add), r=["rt_tv" + sfx, hp], w=[hp + "2"])
            S.c("dve", lambda e: e.tensor_tensor(out=c3(idc[:nt]), in0=i1s[:nt].unsqueeze(2).broadcast_to([nt, 16, 16]), in1=ti[:nt, 16:32].unsqueeze(1).broadcast_to([nt, 16, 16]), op=ALU.add), r=["rt_i1" + sfx, "rt_tf" + sfx, hp], w=[hp + "3"])
            S.c("dve", lambda e: e.max(out=scv[:nt, 0:8], in_=cand[:nt]), r=[hp + "2"], w=["rt_sc" + sfx])
            S.c("dve", lambda e: e.max_index(out=fiu[:nt, 0:8], in_max=scv[:nt, 0:8], in_values=cand[:nt]), r=[hp + "2", "rt_sc" + sfx], w=["rt_fi" + sfx])
            S.c("dve", lambda e: e.match_replace(out=candw[:nt], in_to_replace=scv[:nt, 0:8], in_values=cand[:nt], imm_value=-1e30), r=[hp + "2", "rt_sc" + sfx], w=[hp + "4"])
            S.c("dve", lambda e: e.max(out=scv[:nt, 8:16], in_=candw[:nt]), r=[hp + "4"], w=["rt_sc" + sfx])
            S.c("dve", lambda e: e.max_index(out=fiu[:nt, 8:16], in_max=scv[:nt, 8:16], in_values=candw[:nt]), r=[hp + "4", "rt_sc" + sfx], w=["rt_fi" + sfx])
            S.c("dve", lambda e: e.tensor_copy(ff[:nt], fiu[:nt]), r=["rt_fi" + sfx], w=["rt_ff" + sfx])
            S.c("dve", lambda e, h=h: e.memset(idsf[:nt, h * 16:(h + 1) * 16], 0.0), w=["pr_ids%d" % h])
            junks = sc["junks"]
            for k in range(16):
                jk = junks[k % len(junks)]
                S.c("dve", lambda e, h=h, k=k, jk=jk: e.scalar_tensor_tensor(out=jk[:nt], in0=cst(K_IOTA, 256)[:nt], scalar=ff[:nt, k:k + 1], in1=idc[:nt], op0=ALU.is_equal, op1=ALU.mult, accum_out=idsf[:nt, h * 16 + k:h * 16 + k + 1]), r=[hp + "3", "rt_ff" + sfx, "CST", "pr_ids%d" % h], w=[hp + "5_%d" % (k % len(junks)), "pr_idc%d_%d" % (h, k)])
            S.c("dve", lambda e: e.tensor_scalar(out=nmax[:nt], in0=scv[:nt, 0:1], scalar1=-1.0, scalar2=None, op0=ALU.mult), r=["rt_sc" + sfx], w=["rt_nm" + sfx])
            S.c("dve", lambda e: e.memset(esum[:nt], 0.0), w=["rt_es" + sfx])
            S.c("act", lambda e, h=h: e.activation(out=gsm[:nt, h * 16:(h + 1) * 16], in_=scv[:nt], func=AF.Exp, bias=nmax[:nt], accum_out=esum[:nt]), r=["rt_sc" + sfx, "rt_es" + sfx, "rt_nm" + sfx], w=[gn + "h%d" % h, "rt_es" + sfx])
            S.c("dve", lambda e: e.reciprocal(out=esum[:nt], in_=esum[:nt]), r=["rt_es" + sfx], w=["rt_es" + sfx])
            S.c("dve", lambda e, h=h: e.tensor_scalar(out=gsm[:nt, h * 16:(h + 1) * 16], in0=gsm[:nt, h * 16:(h + 1) * 16], scalar1=esum[:nt], scalar2=None, op0=ALU.mult), r=[gn + "h%d" % h, "rt_es" + sfx], w=[gn + "h%d" % h])

        setA = mkset(H_, RT, [H_[:, 1024:1280], H_[:, 1280:1536], H_[:, 1536:1792], H_[:, 1792:2048]])
        setB = mkset(Y_, Y_[:, 1280:1456], [Y_[:, 1024:1280], Y_[:, 1456:1712], Y_[:, 1712:1968]])
        ops_a = capture(lambda: [route_head(h, setA, "", "H") for h in (0, 2, 4, 6)])
        ops_b = capture(lambda: [route_head(h, setB, "b", "Y") for h in (1, 3, 5, 7)])
        S.ops.extend(merge(ops_a, ops_b))
        S.c("dve", lambda e: e.tensor_copy(IDS[:nt], idsf[:nt]), r=["pr_ids%d" % h for h in range(8)] + ["pr_idc%d_%d" % (h, k) for h in range(8) for k in range(16)], w=[idn])


    def peer_gather(l, T, par):
        nt = T["nt"]
        X = T["X"]
        xname = T["xname"]
        A_, X_, Y_, T_, H_, G_ = [SL[n] for n in ["A", "X", "Y", "T", "H", "G"]]
        ss = SM[:, 0:1]
        hdot = PR[:, 0, :]; wts = PR[:, 1, :]; idsf = PR[:, 2, :]
        gsm = GSMS[par]; IDS = IDSS[par]; xn2 = XN2S[par]
        xn2n = "xn2_%d" % par; idn = "ids%d" % par; gn = "pr_g%d" % par
        ring = RING
        acc = ACC
        HDN = ["pr_hd%d" % i for i in range(128)]
        S.c("dve", lambda e: e.memset(hdot[:nt], 0.0), w=HDN)
        gi = [0]

        def gather(tab, hk):
            s = gi[0] % NS_RING
            gi[0] += 1
            rn = "rg%d" % s
            S.d("pool", lambda e: e.indirect_dma_start(out=ring[:nt, s, :], out_offset=None, in_=tab, in_offset=bass.IndirectOffsetOnAxis(ap=IDS[:nt, hk:hk + 1], axis=0)),
                r=[idn], w=[rn], key=rn)
            return s, rn

        for hk in range(128):
            s, rn = gather(U_d[l], hk)
            S.c("dve", lambda e, s=s, hk=hk: e.scalar_tensor_tensor(out=ring[:nt, s, :], in0=ring[:nt, s, :], scalar=1.0, in1=xn2[:nt], op0=ALU.mult, op1=ALU.mult, accum_out=hdot[:nt, hk:hk + 1]), r=[rn, xn2n, "pr_hd%d" % hk], w=[rn, "pr_hd%d" % hk])
        S.c("act", lambda e: e.activation(out=wts[:nt], in_=hdot[:nt], func=AF.Gelu), r=HDN, w=["pr_w"])
        S.c("dve", lambda e: e.tensor_tensor(out=wts[:nt], in0=wts[:nt], in1=gsm[:nt], op=ALU.mult), r=["pr_w"] + [gn + "h%d" % h for h in range(8)], w=["pr_w"])
        for hk in range(128):
            s, rn = gather(V_d[l], hk)
            if hk == 0:
                S.c("dve", lambda e, s=s: e.tensor_scalar(out=acc[:nt], in0=ring[:nt, s, :], scalar1=wts[:nt, 0:1], scalar2=None, op0=ALU.mult), r=[rn, "pr_w"], w=["acc"])
            else:
                S.c("dve", lambda e, s=s, hk=hk: e.scalar_tensor_tensor(out=acc[:nt], in0=ring[:nt, s, :], scalar=wts[:nt, hk:hk + 1], in1=acc[:nt], op0=ALU.mult, op1=ALU.add), r=[rn, "pr_w", "acc"], w=["acc"])
        S.c("dve", lambda e: e.tensor_tensor(out=X[:nt], in0=X[:nt], in1=acc[:nt], op=ALU.add), r=[xname, "acc"], w=[xname])

    def final_out(T):
        nt = T["nt"]
        X = T["X"]
        xname = T["xname"]
        A_ = SL["A"]; G_ = SL["G"]
        ss = SM[:, 0:1]
        fn = G_[:, 1024:2048]
        S.d("sp", lambda e: e.dma_start(out=fn, in_=fnw_d[0:1, :].broadcast_to([128, D])), w=["G"], key="ldG")
        S.c("dve", lambda e: e.memset(ss[:nt], 0.0), w=["sm_ss"])
        S.c("dve", lambda e: e.scalar_tensor_tensor(out=A_[:nt, 0:1024], in0=X[:nt], scalar=1.0, in1=X[:nt], op0=ALU.mult, op1=ALU.mult, accum_out=ss[:nt]), r=[xname], w=["A", "sm_ss"])
        S.c("act", lambda e: e.activation(out=ss[:nt], in_=ss[:nt], func=AF.Sqrt, bias=EPS[:nt], scale=1.0 / 1024.0), r=["sm_ss", "CST"], w=["sm_ss"])
        S.c("dve", lambda e: e.reciprocal(out=ss[:nt], in_=ss[:nt]), r=["sm_ss"], w=["sm_ss"])
        S.c("dve", lambda e: e.scalar_tensor_tensor(out=A_[:nt, 0:1024], in0=X[:nt], scalar=ss[:nt], in1=fn[:nt], op0=ALU.mult, op1=ALU.mult), r=[xname, "sm_ss", "G"], w=["A"])
        S.d("sp", lambda e: e.dma_start(out=T["out"], in_=A_[:nt, 0:1024]), r=["A"], key="oy")

    tiles = [dict(kind="P", nt=16, nseq=1, L=16, src=meta_d, out=None)]
    for i in range(16):
        tiles.append(dict(kind="P", nt=128, nseq=1, L=128, src=xp_d[i * 128:(i + 1) * 128, :], out=yp_d[i * 128:(i + 1) * 128, :]))
    tiles = tiles[:n_ptiles]
    if n_ptiles == 17:
        tiles[-1]["last"] = True
    if do_sample:
        for hf in range(2):
            tiles.append(dict(kind="S", nt=64, nseq=8, L=8, s0=hf * 8, src=xs_d[hf * 64:(hf + 1) * 64, :], out=ys_d[hf * 64:(hf + 1) * 64, :]))
    def capture(fn):
        saved = S.ops
        S.ops = []
        fn()
        out = S.ops
        S.ops = saved
        return out

    def merge(a, b):
        out = []
        na, nb = len(a), len(b)
        if na == 0 or nb == 0:
            return a + b
        ia = 0
        for ib, op in enumerate(b):
            tgt = (ib * na) // nb
            while ia < tgt:
                out.append(a[ia]); ia += 1
            out.append(op)
        out.extend(a[ia:])
        return out

    steps = []
    ptl = [T for T in tiles if T["kind"] == "P"]
    stl = [T for T in tiles if T["kind"] == "S"]
    groups = [ptl[0:1]] + [ptl[i:i + 2] for i in range(1, len(ptl), 2)] + [stl[i:i + 2] for i in range(0, len(stl), 2)]
    for grp in groups:
        for l in range(2):
            for T in grp:
                steps.append((T, l))
    for ti, T in enumerate(tiles):
        T["X"] = XB[ti % 2]
        T["xname"] = "x%d" % (ti % 2)

    def mixer_route_ops(T, l, par):
        def f():
            if l == 0:
                S.d("sp", lambda e: e.dma_start(out=T["X"][:T["nt"]], in_=T["src"]), w=[T["xname"]], key="ld" + T["xname"])
            layer_step(l, T)
            peer_route(l, T, par)
        return capture(f)

    S.ops.extend(mixer_route_ops(steps[0][0], steps[0][1], 0))
    for k, (T, l) in enumerate(steps):
        g = capture(lambda: peer_gather(l, T, k % 2))
        if k + 1 < len(steps):
            T2, l2 = steps[k + 1]
            m = mixer_route_ops(T2, l2, (k + 1) % 2)
            if T2 is not T:
                S.ops.extend(merge(g, m))
            else:
                S.ops.extend(g)
                S.ops.extend(m)
        else:
            S.ops.extend(g)
        if l == 1 and T["out"] is not None:
            final_out(T)
    S.build()
    es.close()
    return nc, S


def _pack_chp(inp, l):
    t = lambda v: np.ascontiguousarray(v.reshape(-1, 128).T)
    b_in = inp["b_in"][l]
    cw = inp["conf_dw_w"][l]
    mw = inp["m_conv_w"][l]
    cols = [t(b_in[0:2048]), t(b_in[4096:8192]),
            np.ascontiguousarray(cw.T.reshape(8, 128, 31).transpose(1, 0, 2).reshape(128, 248)),
            t(inp["conf_dw_b"][l]), t(inp["conf_ln_g"][l]), t(inp["conf_ln_b"][l]),
            np.ascontiguousarray(mw.T.reshape(32, 128, 4).transpose(1, 0, 2).reshape(128, 128)),
            t(inp["m_conv_b"][l]), t(inp["norm1_w"][l]), t(inp["m_norm_w"][l])]
    out = np.concatenate(cols, axis=1).astype(np.float32)
    assert out.shape == (128, NCH)
    return out


def make_in_maps(inp, n_cores=8):
    f = lambda a: np.ascontiguousarray(np.asarray(a, dtype=np.float32))
    inp = {k: np.asarray(v) for k, v in inp.items()}
    chp = np.stack([_pack_chp(inp, 0), _pack_chp(inp, 1)])
    keysT = f(inp["peer_keys"].reshape(2, 16, 128, 128).transpose(0, 3, 1, 2))
    shared = {
        "meta": f(inp["meta_tokens"]), "w_in": f(inp["w_in"]), "b_in": f(inp["b_in"]),
        "w_conf_out": f(inp["w_conf_out"]), "b_conf_out": f(inp["b_conf_out"]),
        "w_m_out": f(inp["w_m_out"]), "w_o": f(inp["w_o"]), "peer_w_q": f(inp["peer_w_q"]),
        "keysT": keysT, "peer_u0": f(inp["peer_u"][0]), "peer_u1": f(inp["peer_u"][1]), "peer_v0": f(inp["peer_v"][0]), "peer_v1": f(inp["peer_v"][1]),
        "chp": chp, "consts": _consts(), "norm2_w": f(inp["norm2_w"]),
        "final_norm_w": f(inp["final_norm_w"].reshape(1, D)), "A_log": f(inp["A_log"]),
        "D_skip": f(inp["D_skip"]), "dt_bias": f(inp["dt_bias"]),
    }
    maps = []
    for c in range(n_cores):
        m = dict(shared)
        m["xp"] = f(inp["x_prompt"][c])
        m["xs"] = f(inp["x_sample"][16 * c:16 * c + 16].reshape(128, D))
        m["cconf"] = f(inp["cache_conf"][:, 16 * c:16 * c + 16])
        m["cmconv"] = f(inp["cache_mconv"][:, 16 * c:16 * c + 16])
        m["sssm"] = f(inp["state_ssm"][:, 16 * c:16 * c + 16].reshape(2, 16, 2048, 128))
        maps.append(m)
    return maps


_PROG = {}


def kernel(**inputs):
    if "p" not in _PROG:
        _PROG["p"] = build_program()
    nc, _ = _PROG["p"]
    maps = make_in_maps(inputs)
    res = run_bass_kernel_spmd(nc, maps, core_ids=list(range(8)))
    R = res.results
    cat = lambda k, ax=0: np.concatenate([np.asarray(r[k], dtype=np.float32)[None] if ax is None else np.asarray(r[k], dtype=np.float32) for r in R], axis=0)
    y_prompt = np.stack([np.asarray(r["y_prompt"], np.float32) for r in R])
    y_sample = np.concatenate([np.asarray(r["y_sample"], np.float32).reshape(16, 8, D) for r in R], axis=0)
    conf_p = np.stack([np.asarray(r["conf_p"], np.float32) for r in R], axis=1)
    mconv_p = np.stack([np.asarray(r["mconv_p"], np.float32) for r in R], axis=1)
    ssm_p = np.stack([np.asarray(r["ssm_p"], np.float32).reshape(2, 32, 64, 128) for r in R], axis=1)
    conf_s = np.concatenate([np.asarray(r["conf_s"], np.float32) for r in R], axis=1)
    mconv_s = np.concatenate([np.asarray(r["mconv_s"], np.float32) for r in R], axis=1)
    ssm_s = np.concatenate([np.asarray(r["ssm_s"], np.float32).reshape(2, 16, 32, 64, 128) for r in R], axis=1)
    return (y_prompt, y_sample, conf_p, mconv_p, ssm_p, conf_s, mconv_s, ssm_s)
```

```python
import numpy as np
from contextlib import ExitStack
import concourse.bass as bass
import concourse.mybir as mybir
from concourse.bass_utils import run_bass_kernel_spmd

F32 = mybir.dt.float32
I32 = mybir.dt.int32
U32 = mybir.dt.uint32
AF = mybir.ActivationFunctionType
ALU = mybir.AluOpType
AX = mybir.AxisListType

ENGS = ("pe", "dve", "act", "pool", "sp")


class Sched:
    def __init__(self, nc):
        self.nc = nc
        self.ops = []

    def c(self, eng, fn, r=(), w=(), quiet=False):
        self.ops.append(("q" if quiet else "c", eng, fn, tuple(r), tuple(w), None))

    def d(self, eng, fn, r=(), w=(), key=None):
        self.ops.append(("d", eng, fn, tuple(r), tuple(w), key))

    def build(self):
        nc = self.nc
        cnt = {e: 0 for e in ENGS}
        dcnt = {}
        lastw = {}
        readers = {}
        seen = {e: {} for e in ENGS}
        streams = {e: [] for e in ENGS}
        for kind, eng, fn, r, w, key in self.ops:
            deps = {}

            def add(tok):
                if tok is None:
                    return
                s, v = tok[0], tok[1]
                if len(tok) > 2 and s == "E:" + eng:
                    return
                if deps.get(s, 0) < v:
                    deps[s] = v

            for b in r:
                add(lastw.get(b))
            for b in w:
                add(lastw.get(b))
                for t in readers.get(b, ()):
                    add(t)
            if kind == "d":
                sname = "D:" + key
                if dcnt.get(key, 0) > 0:
                    add((sname, 16 * dcnt[key]))
                dcnt[key] = dcnt.get(key, 0) + 1
                tok = (sname, 16 * dcnt[key])
            elif kind == "q":
                tok = ("E:" + eng, cnt[eng] + 1, "q")
            else:
                cnt[eng] += 1
                tok = ("E:" + eng, cnt[eng])
            waits = []
            for s, v in deps.items():
                if seen[eng].get(s, 0) < v:
                    seen[eng][s] = v
                    waits.append((s, v))
            streams[eng].append((waits, fn, None if kind == "q" else tok))
            for b in r:
                readers.setdefault(b, []).append(tok)
            for b in w:
                lastw[b] = tok
                readers[b] = []
        final_waits = [("D:" + k, 16 * n) for k, n in dcnt.items()]
        final_eng = [("E:" + e, cnt[e]) for e in ENGS if cnt[e] > 0]
        semnames = ["E:" + e for e in ENGS if cnt[e] > 0] + ["D:" + k for k in dcnt]
        self.n_sems = len(semnames)
        self.counts = dict(cnt)
        with ExitStack() as es:
            sems = {}
            for i, s in enumerate(semnames):
                sems[s] = es.enter_context(nc.semaphore("s%d" % i))
            block = es.enter_context(nc.Block())

            def mk(ename):
                def body(e):
                    for waits, fn, tok in streams[ename]:
                        for s, v in waits:
                            e.wait_ge(sems[s], v)
                        ins = fn(e)
                        if tok is None:
                            continue
                        s, v = tok
                        ins.then_inc(sems[s], 16 if s.startswith("D:") else 1)
                    if ename == "sp":
                        for s, v in final_waits + final_eng:
                            e.wait_ge(sems[s], v)
                return body

            if streams["pe"]:
                block.tensor(mk("pe"))
            if streams["dve"]:
                block.vector(mk("dve"))
            if streams["act"]:
                block.scalar(mk("act"))
            if streams["pool"]:
                block.gpsimd(mk("pool"))
            block.sync(mk("sp"))


D = 1024
NIN = 10272
NCH = 504
C_BG, C_BX, C_CW, C_CB, C_LG, C_LB, C_MW, C_MB, C_N1, C_MN = 0, 16, 48, 296, 304, 312, 320, 448, 480, 488
K_ID, K_UP, K_GP, K_US, K_GS, K_ONE, K_ONEC, K_SI, K_EPS, K_IOTA, K_NCONST = 0, 128, 256, 384, 512, 640, 768, 896, 904, 912, 1168
NS_RING = 3


def _consts():
    c = np.zeros((128, K_NCONST), np.float32)
    p = np.arange(128)
    c[:, K_ID:K_ID + 128] = (p[:, None] == p[None, :])
    c[:, K_UP:K_UP + 128] = (p[:, None] <= p[None, :])
    c[:, K_GP:K_GP + 128] = (p[:, None] > p[None, :])
    same = (p[:, None] // 8) == (p[None, :] // 8)
    c[:, K_US:K_US + 128] = (p[:, None] <= p[None, :]) & same
    c[:, K_GS:K_GS + 128] = (p[:, None] > p[None, :]) & same
    c[:, K_ONE:K_ONE + 128] = 1.0
    c[:, K_ONEC:K_ONEC + 128] = 1.0 / 1024.0
    c[:, K_SI:K_SI + 8] = (p[:, None] // 8) == np.arange(8)[None, :]
    c[:, K_EPS] = 1e-6
    c[:, K_EPS + 1] = 1.0
    c[:, K_EPS + 2] = 0.0
    c[:, K_IOTA:K_IOTA + 256] = np.arange(256, dtype=np.float32)[None, :]
    return c


def build_program(n_ptiles=17, do_sample=True):
    nc = bass.Bass("TRN2", target_bir_lowering=False)
    din = lambda n, s, dt=F32: nc.dram_tensor(n, s, dt, kind="ExternalInput").ap()
    dout = lambda n, s, dt=F32: nc.dram_tensor(n, s, dt, kind="ExternalOutput").ap()
    xp_d = din("xp", [2048, D]); meta_d = din("meta", [16, D]); xs_d = din("xs", [128, D])
    cconf_d = din("cconf", [2, 16, 30, 1024]); cmconv_d = din("cmconv", [2, 16, 3, 4096])
    sssm_d = din("sssm", [2, 16, 2048, 128])
    w_in_d = din("w_in", [2, D, NIN]); b_in_d = din("b_in", [2, NIN])
    w_co_d = din("w_conf_out", [2, D, D]); b_co_d = din("b_conf_out", [2, D])
    w_mo_d = din("w_m_out", [2, 2048, D]); w_o_d = din("w_o", [2, D, D])
    w_q_d = din("peer_w_q", [2, D, 2048]); keysT_d = din("keysT", [2, 128, 16, 128])
    U_d = [din("peer_u0", [16384, D]), din("peer_u1", [16384, D])]; V_d = [din("peer_v0", [16384, D]), din("peer_v1", [16384, D])]
    chp_d = din("chp", [2, 128, NCH]); consts_d = din("consts", [128, K_NCONST])
    n2w_d = din("norm2_w", [2, D]); fnw_d = din("final_norm_w", [1, D])
    alog_d = din("A_log", [2, 32]); dsk_d = din("D_skip", [2, 32]); dtb_d = din("dt_bias", [2, 32])
    yp_d = dout("y_prompt", [2048, D]); ys_d = dout("y_sample", [128, D])
    confp_d = dout("conf_p", [2, 30, 1024]); mconvp_d = dout("mconv_p", [2, 3, 4096])
    ssmp_d = dout("ssm_p", [2, 2048, 128])
    confs_d = dout("conf_s", [2, 16, 30, 1024]); mconvs_d = dout("mconv_s", [2, 16, 3, 4096])
    ssms_d = dout("ssm_s", [2, 16, 2048, 128])

    es = ExitStack()
    sb = lambda n, s, dt=F32: es.enter_context(nc.sbuf_tensor(n, s, dt))
    XB = [sb("x0", [128, D]), sb("x1", [128, D])]
    XN2S = [sb("xn2_0", [128, D]), sb("xn2_1", [128, D])]
    GSMS = [sb("gsm0", [128, 128]), sb("gsm1", [128, 128])]
    ACC = sb("acc", [128, D])
    RING = sb("ring", [128, NS_RING, D])
    Hs = [sb("h0", [128, 16, 128]), sb("h1", [128, 16, 128])]
    histA = [sb("histA0", [128, 8, 30]), sb("histA1", [128, 8, 30])]
    histM = [sb("histM0", [128, 32, 3]), sb("histM1", [128, 32, 3])]
    CST = sb("consts_sb", [128, K_NCONST])
    CHP = [sb("chp0", [128, NCH]), sb("chp1", [128, NCH])]
    AB = [sb("ab0", [128, 32]), sb("ab1", [128, 32])]
    DK = [sb("dk0", [128, 32]), sb("dk1", [128, 32])]
    WB = [sb("wb%d" % i, [128, 9, 256]) for i in range(3)]
    DTB = sb("dtb", [1, 64])
    SM = sb("small", [128, 512])
    DECS = sb("decS", [128, 16, 8])
    IDSS = [sb("ids_i0", [128, 128], I32), sb("ids_i1", [128, 128], I32)]
    PR = sb("peer_small", [128, 3, 128])
    RT = sb("route_small", [128, 176])
    CM = sb("cm", [128, 6656])
    SL = {n: sb("slot" + n, [128, 2048]) for n in ["A", "X", "Bc", "R", "E", "Y", "T", "H", "Z", "G", "C"]}
    ps = lambda n, s: es.enter_context(nc.psum_tensor(n, s, F32))
    PA = ps("pA", [128, 1024]); PB = ps("pB", [128, 1024]); PY = ps("pY", [128, 2048])
    banks = [(PA, 0, "pA0"), (PA, 1, "pA1"), (PB, 0, "pB0"), (PB, 1, "pB1")]

    def bank(i):
        t, k, n = banks[i]
        return t[:, k * 512:(k + 1) * 512], n

    ybank = lambda i: (PY[:, i * 512:(i + 1) * 512], "pY%d" % i)
    YN = ["pY0", "pY1", "pY2", "pY3"]

    S = Sched(nc)
    YALL = ["Y", "Y2", "Y3", "Y4", "Y5"] + ["rt_%sb" % k_ for k_ in ("tv", "ti", "tf", "i1", "sc", "fi", "ff", "nm", "es")]
    cst = lambda col, n=128: CST[:, col:col + n]
    ident = cst(K_ID)
    EPS = CST[:, K_EPS:K_EPS + 1]
    ONE1 = CST[:, K_EPS + 1:K_EPS + 2]
    ZERO1 = CST[:, K_EPS + 2:K_EPS + 3]

    S.d("sp", lambda e: e.dma_start(out=CST[:], in_=consts_d), w=["CST"], key="cst")
    for l in range(2):
        S.d("sp", lambda e, l=l: e.dma_start(out=CHP[l][:], in_=chp_d[l]), w=["chp%d" % l], key="chp%d" % l)
        S.d("sp", lambda e, l=l: e.dma_start(out=AB[l][:], in_=alog_d[l:l + 1, :].broadcast_to([128, 32])), w=["ab%d" % l], key="ab%d" % l)
        S.d("sp", lambda e, l=l: e.dma_start(out=DK[l][:], in_=dsk_d[l:l + 1, :].broadcast_to([128, 32])), w=["dk%d" % l], key="dk%d" % l)
        S.c("act", lambda e, l=l: e.activation(out=AB[l][:], in_=AB[l][:], func=AF.Exp), r=["ab%d" % l], w=["ab%d" % l])
        S.c("dve", lambda e, l=l: e.tensor_scalar(out=AB[l][:], in0=AB[l][:], scalar1=-1.0, scalar2=None, op0=ALU.mult), r=["ab%d" % l], w=["ab%d" % l])
        S.c("pool", lambda e, l=l: e.memset(Hs[l][:], 0.0), w=["h%d" % l])
        S.c("pool", lambda e, l=l: e.memset(histA[l][:], 0.0), w=["histA%d" % l])
        S.c("pool", lambda e, l=l: e.memset(histM[l][:], 0.0), w=["histM%d" % l])

    wb_i = [0]

    def wload(src_ap, ncols, bias_ap=None):
        i = wb_i[0] % 3
        wb_i[0] += 1
        wn = "wb%d" % i
        S.d("sp", lambda e: e.dma_start(out=WB[i][:, 0:8, 0:ncols], in_=src_ap.rearrange("(k p) c -> p k c", p=128)), w=[wn], key=wn)
        if bias_ap is not None:
            S.d("sp", lambda e: e.dma_start(out=WB[i][0:1, 8, 0:ncols], in_=bias_ap), w=[wn + "b"], key=wn + "b")
        return WB[i], wn

    bk_i = [0]
    bank_pool = [0, 1, 2, 3]

    def nextbank():
        b = bank(bank_pool[bk_i[0] % len(bank_pool)])
        bk_i[0] += 1
        return b

    def transpose_into(dst_fn, src_fn, n_items, np_in, nf_in, dst_names, src_names, evac=None):
        per = 4
        for b0 in range(0, n_items, per):
            bt, bn = nextbank()
            n = min(per, n_items - b0)
            for q in range(n):
                j = b0 + q
                S.c("pe", lambda e, j=j, q=q, bt=bt: e.transpose(bt[:nf_in, q * 128:q * 128 + np_in], src_fn(j), ident[:np_in, :np_in]),
                    r=list(src_names) + ["CST"], w=[bn], quiet=(q < n - 1))
            dst_fn(b0, n, bt, bn)

    def layer_step(l, T):
        nt, nseq, L = T["nt"], T["nseq"], T["L"]
        samp = T["kind"] == "S"
        Umask = cst(K_US) if samp else cst(K_UP)
        Gseq = cst(K_GS) if samp else cst(K_GP)
        Gmask = cst(K_GP)
        chp = CHP[l]
        cn = "chp%d" % l
        cat = CM[:, 0:8 * nseq * (30 + L)].rearrange("p (j b t) -> p j b t", j=8, b=nseq)
        mcat = CM[:, 2464:2464 + 32 * nseq * (3 + L)].rearrange("p (j b t) -> p j b t", j=32, b=nseq)
        A_, X_, Bc_, R_, E_, Y_, T_, H_, Z_, G_, C_ = [SL[n] for n in ["A", "X", "Bc", "R", "E", "Y", "T", "H", "Z", "G", "C"]]
        X = T["X"]
        xname = T["xname"]
        xn = A_[:, 0:1024]
        xnT = A_[:, 1024:2048].rearrange("p (k t) -> p k t", k=8)
        v3 = lambda ap, b: ap.rearrange("p (b l) -> p b l", b=b)
        catn = lambda j: "cat%d" % j
        mcatn = lambda j: "mcat%d" % j
        ALLCAT = [catn(j) for j in range(8)]
        ALLMCAT = [mcatn(j) for j in range(32)]
        early = not (samp or T.get("last"))
        cacc = R_[:, 0:1024].rearrange("p (j t) -> p j t", j=8)
        csq = R_[:, 1024:2048].rearrange("p (j t) -> p j t", j=8)
        xcx = T_.rearrange("p (j t) -> p j t", j=16)
        xcb = C_.rearrange("p (j t) -> p j t", j=16)

        def conv_a_tile(j):
            acc = v3(cacc[:, j, :nt], nseq)
            S.c("dve", lambda e: e.tensor_scalar(out=acc, in0=cat[:, j, :, 0:L], scalar1=chp[:, C_CW + j * 31:C_CW + j * 31 + 1], scalar2=chp[:, C_CB + j:C_CB + j + 1], op0=ALU.mult, op1=ALU.add), r=[catn(j), cn], w=["R"])
            for k in range(1, 31):
                S.c("dve", lambda e, k=k: e.scalar_tensor_tensor(out=acc, in0=cat[:, j, :, k:k + L], scalar=chp[:, C_CW + j * 31 + k:C_CW + j * 31 + k + 1], in1=acc, op0=ALU.mult, op1=ALU.add), r=[catn(j), cn, "R"], w=["R"])

        def conv_b_tile(j):
            dst = (xcx if j < 16 else xcb)
            sn = "T" if j < 16 else "C"
            acc = v3(dst[:, j % 16, :nt], nseq)
            S.c("dve", lambda e: e.tensor_scalar(out=acc, in0=mcat[:, j, :, 0:L], scalar1=chp[:, C_MW + j * 4:C_MW + j * 4 + 1], scalar2=chp[:, C_MB + j:C_MB + j + 1], op0=ALU.mult, op1=ALU.add), r=[mcatn(j), cn], w=[sn])
            for k in range(1, 4):
                S.c("dve", lambda e, k=k: e.scalar_tensor_tensor(out=acc, in0=mcat[:, j, :, k:k + L], scalar=chp[:, C_MW + j * 4 + k:C_MW + j * 4 + k + 1], in1=acc, op0=ALU.mult, op1=ALU.add), r=[mcatn(j), cn, sn], w=[sn])

        if samp:
            s0 = T["s0"]
            for q4 in range(nseq // 4):
                S.d("sp", lambda e, q4=q4: e.dma_start(out=Y_[0:120, 0:1024], in_=cconf_d[l, s0 + 4 * q4:s0 + 4 * q4 + 4].rearrange("b r c -> (b r) c")), w=YALL, key="ldY")
                for j0 in range(0, 8, 4):
                    bt, bn = nextbank()
                    for q in range(4):
                        j = j0 + q
                        S.c("pe", lambda e, j=j, q=q, bt=bt: e.transpose(bt[:, q * 128:q * 128 + 120], Y_[0:120, j * 128:(j + 1) * 128], ident[:120, :120]), r=["Y", "CST"], w=[bn], quiet=(q < 3))
                    S.c("act", lambda e, j0=j0, bt=bt, q4=q4: e.activation(out=cat[:, j0:j0 + 4, 4 * q4:4 * q4 + 4, 0:30],
                        in_=bt[:, :].rearrange("p (q c) -> p q c", q=4)[:, :, 0:120].rearrange("p q (b r) -> p q b r", b=4), func=AF.Identity), r=[bn], w=[catn(j0 + q_) for q_ in range(4)])
            S.d("sp", lambda e: e.dma_start(out=Y_[0:3 * nseq, 0:2048], in_=cmconv_d[l, s0:s0 + nseq, :, 0:2048].rearrange("b r c -> (b r) c")), w=YALL, key="ldY")
            S.d("sp", lambda e: e.dma_start(out=T_[0:3 * nseq, 0:2048], in_=cmconv_d[l, s0:s0 + nseq, :, 2048:4096].rearrange("b r c -> (b r) c")), w=["T"], key="ldT")
            nr = 3 * nseq
            for half, src, sname in ((0, Y_, "Y"), (1, T_, "T")):
                for j0 in range(0, 16, 4):
                    bt, bn = nextbank()
                    for q in range(4):
                        j = j0 + q
                        S.c("pe", lambda e, j=j, q=q, bt=bt, src=src: e.transpose(bt[:, q * 128:q * 128 + nr], src[0:nr, j * 128:(j + 1) * 128], ident[:nr, :nr]), r=[sname, "CST"], w=[bn], quiet=(q < 3))
                    jj = half * 16 + j0
                    S.c("act", lambda e, jj=jj, bt=bt: e.activation(out=mcat[:, jj:jj + 4, :, 0:3],
                        in_=bt[:, :].rearrange("p (q c) -> p q c", q=4)[:, :, 0:nr].rearrange("p q (b r) -> p q b r", b=nseq), func=AF.Identity), r=[bn], w=[mcatn(jj + q_) for q_ in range(4)])
        else:
            S.c("pool", lambda e: e.tensor_copy(cat[:, :, 0, 0:30], histA[l][:]), r=["histA%d" % l], w=ALLCAT)
            S.c("pool", lambda e: e.tensor_copy(mcat[:, :, 0, 0:3], histM[l][:]), r=["histM%d" % l], w=ALLMCAT)

        ss = SM[:, 0:1]
        S.c("dve", lambda e: e.memset(ss[:nt], 0.0), w=["sm_ss"])
        S.c("dve", lambda e: e.scalar_tensor_tensor(out=xn[:nt], in0=X[:nt], scalar=1.0, in1=X[:nt], op0=ALU.mult, op1=ALU.mult, accum_out=ss[:nt]), r=[xname], w=["A", "sm_ss"])
        S.c("act", lambda e: e.activation(out=ss[:nt], in_=ss[:nt], func=AF.Sqrt, bias=EPS[:nt], scale=1.0 / 1024.0), r=["sm_ss", "CST"], w=["sm_ss"])
        S.c("dve", lambda e: e.reciprocal(out=ss[:nt], in_=ss[:nt]), r=["sm_ss"], w=["sm_ss"])
        S.c("dve", lambda e: e.tensor_scalar(out=xn[:nt], in0=X[:nt], scalar1=ss[:nt], scalar2=None, op0=ALU.mult), r=[xname, "sm_ss"], w=["A"])
        for k0 in range(0, 8, 4):
            bt, bn = nextbank()
            for q in range(4):
                k = k0 + q
                S.c("pe", lambda e, k=k, q=q, bt=bt: e.transpose(bt[:, q * 128:q * 128 + nt], xn[:nt, k * 128:(k + 1) * 128], ident[:nt, :nt]), r=["A", "CST"], w=[bn], quiet=(q < 3))
            for q in range(4):
                k = k0 + q
                S.c("act", lambda e, k=k, q=q, bt=bt: e.activation(out=xnT[:, k, :nt], in_=bt[:, q * 128:q * 128 + nt], func=AF.Identity, scale=chp[:, C_N1 + k:C_N1 + k + 1]), r=[bn, cn], w=["A2"])

        BW = 256
        zs = Z_
        gates = G_
        dtr = SM[:, 64:96]
        sgt = [SM[:, 256:384], SM[:, 384:512]]
        segs = [("glu", 0, 2048), ("z", 2048, 4096), ("xbc", 4096, 8192), ("dt", 8192, 8224), ("gate", 8224, 10272)]
        for sname, cs0, cs1 in segs:
            for c0 in range(cs0, cs1, BW):
                ncols = min(BW, cs1 - c0)
                chan_major = sname in ("glu", "xbc")
                wbt, wn = wload(w_in_d[l, :, c0:c0 + ncols], ncols, None if chan_major else b_in_d[l:l + 1, c0:c0 + ncols])
                if chan_major:
                    for f in range(ncols // 128):
                        bt, bn = nextbank()
                        for k in range(8):
                            S.c("pe", lambda e, f=f, k=k, bt=bt, wbt=wbt: e.matmul(bt[:, :nt], wbt[:, k, f * 128:(f + 1) * 128], xnT[:, k, :nt], start=(k == 0), stop=(k == 7)), r=[wn, "A2"], w=[bn], quiet=(k < 7))
                        fg = (c0 - cs0) // 128 + f
                        if sname == "glu":
                            if fg < 8:
                                S.c("act", lambda e, fg=fg, bt=bt: e.activation(out=cat[:, fg, :, 30:30 + L], in_=v3(bt[:, :nt], nseq), func=AF.Identity, bias=chp[:, C_BG + fg:C_BG + fg + 1]), r=[bn, cn], w=[catn(fg)])
                            else:
                                j = fg - 8
                                sg = sgt[j % 2]
                                sgn = "sg%d" % (j % 2)
                                S.c("act", lambda e, fg=fg, bt=bt, sg=sg: e.activation(out=sg[:, :nt], in_=bt[:, :nt], func=AF.Sigmoid, bias=chp[:, C_BG + fg:C_BG + fg + 1]), r=[bn, cn], w=[sgn])
                                S.c("dve", lambda e, j=j, sg=sg: e.tensor_tensor(out=cat[:, j, :, 30:30 + L], in0=cat[:, j, :, 30:30 + L], in1=v3(sg[:, :nt], nseq), op=ALU.mult), r=[catn(j), sgn], w=[catn(j)])
                                if early:
                                    conv_a_tile(j)
                        else:
                            j = fg
                            S.c("act", lambda e, j=j, bt=bt: e.activation(out=mcat[:, j, :, 3:3 + L], in_=v3(bt[:, :nt], nseq), func=AF.Identity, bias=chp[:, C_BX + j:C_BX + j + 1]), r=[bn, cn], w=[mcatn(j)])
                            if early:
                                conv_b_tile(j)
                else:
                    bt, bn = nextbank()
                    for k in range(8):
                        S.c("pe", lambda e, k=k, bt=bt, wbt=wbt, ncols=ncols: e.matmul(bt[:nt, :ncols], xnT[:, k, :nt], wbt[:, k, 0:ncols], start=(k == 0), stop=False), r=[wn, "A2"], w=[bn], quiet=True)
                    if sname == "dt":
                        S.d("sp", lambda e: e.dma_start(out=DTB[0:1, 0:32], in_=dtb_d[l:l + 1, :]), w=["dtb"], key="dtb")
                        S.c("pe", lambda e, bt=bt: e.matmul(bt[:nt, :32], cst(K_ONE)[0:1, :nt], DTB[0:1, 0:32], start=False, stop=False), r=["dtb", "CST"], w=[bn])
                    S.c("pe", lambda e, bt=bt, wbt=wbt, ncols=ncols: e.matmul(bt[:nt, :ncols], cst(K_ONE)[0:1, :nt], wbt[0:1, 8, 0:ncols], start=False, stop=True), r=[wn + "b", "CST"], w=[bn])
                    cc = c0 - cs0
                    if sname == "z":
                        S.c("act", lambda e, cc=cc, bt=bt, ncols=ncols: e.activation(out=zs[:nt, cc:cc + ncols], in_=bt[:nt, :ncols], func=AF.Silu), r=[bn], w=["Z"])
                    elif sname == "dt":
                        S.c("act", lambda e, bt=bt: e.activation(out=dtr[:nt], in_=bt[:nt, 0:32], func=AF.Exp), r=[bn], w=["sm_dt"])
                        S.c("act", lambda e: e.activation(out=dtr[:nt], in_=dtr[:nt], func=AF.Ln, bias=ONE1[:nt]), r=["sm_dt", "CST"], w=["sm_dt"])
                    else:
                        S.c("act", lambda e, cc=cc, bt=bt, ncols=ncols: e.activation(out=gates[:nt, cc:cc + ncols], in_=bt[:nt, :ncols], func=AF.Sigmoid), r=[bn], w=["G"])

        T["_inproj_end"] = len(S.ops)
        if samp or T.get("last"):
            utm = Y_[:, 0:1024]
            def evac_u(b0, n, bt, bn):
                S.c("act", lambda e: e.activation(out=utm[:nt, b0 * 128:(b0 + n) * 128], in_=bt[:nt, 0:n * 128], func=AF.Identity), r=[bn], w=["Y"])
            ust = E_[:, 0:1024].rearrange("p (j t) -> p j t", j=8)
            S.c("pool", lambda e: e.tensor_copy(ust[:, :, :nt].rearrange("p j (b t) -> p j b t", b=nseq), cat[:, :, :, 30:30 + L]), r=ALLCAT, w=["E"])
            transpose_into(evac_u, lambda j: ust[:, j, :nt], 8, 128, nt, ["Y"], ["E"])
            if samp:
                s0 = T["s0"]
                for b in range(nseq):
                    S.d("sp", lambda e, b=b: e.dma_start(out=confs_d[l, s0 + b, 22:30, :], in_=utm[b * 8:(b + 1) * 8, :]), r=["Y"], key="oconf")
                S.d("sp", lambda e: e.dma_start(out=confs_d[l, s0:s0 + nseq, 0:22, :], in_=cconf_d[l, s0:s0 + nseq, 8:30, :]), key="oconf2")
            else:
                S.d("sp", lambda e: e.dma_start(out=confp_d[l], in_=utm[nt - 30:nt, :]), r=["Y"], key="oconf")
            xst = T_.rearrange("p (j t) -> p j t", j=16)
            for half in range(2):
                S.c("pool", lambda e, half=half: e.tensor_copy(xst[:, :, :nt].rearrange("p j (b t) -> p j b t", b=nseq), mcat[:, half * 16:(half + 1) * 16, :, 3:3 + L]), r=[mcatn(half * 16 + t_) for t_ in range(16)], w=["T"])
                xtm = H_
                def evac_x(b0, n, bt, bn):
                    S.c("act", lambda e: e.activation(out=xtm[:nt, b0 * 128:(b0 + n) * 128], in_=bt[:nt, 0:n * 128], func=AF.Identity), r=[bn], w=["H"])
                transpose_into(evac_x, lambda j: xst[:, j, :nt], 16, 128, nt, ["H"], ["T"])
                if samp:
                    for b in range(nseq):
                        S.d("sp", lambda e, b=b, half=half: e.dma_start(out=mconvs_d[l, s0 + b, :, half * 2048:(half + 1) * 2048], in_=xtm[b * 8 + 5:b * 8 + 8, :]), r=["H"], key="omconv")
                else:
                    S.d("sp", lambda e, half=half: e.dma_start(out=mconvp_d[l, :, half * 2048:(half + 1) * 2048], in_=xtm[nt - 3:nt, :]), r=["H"], key="omconv")
        if not samp:
            S.c("pool", lambda e: e.tensor_copy(histA[l][:], cat[:, :, 0, L:L + 30]), r=ALLCAT, w=["histA%d" % l])

        mark_a = len(S.ops)
        bank_pool[:] = [0, 1]
        if not early:
            for j in range(8):
                conv_a_tile(j)
        S.c("act", lambda e: e.activation(out=csq[:, :, :nt], in_=cacc[:, :, :nt], func=AF.Square), r=["R"], w=["R2"])
        bm, bmn = nextbank()
        bq, bqn = nextbank()
        for j in range(8):
            S.c("pe", lambda e, j=j: e.matmul(bm[:, :nt], cst(K_ONEC), cacc[:, j, :nt], start=(j == 0), stop=(j == 7)), r=["R", "CST"], w=[bmn], quiet=(j < 7))
        for j in range(8):
            S.c("pe", lambda e, j=j: e.matmul(bq[:, :nt], cst(K_ONEC), csq[:, j, :nt], start=(j == 0), stop=(j == 7)), r=["R2", "CST"], w=[bqn], quiet=(j < 7))
        cs = E_[:, 0:1024].rearrange("p (j t) -> p j t", j=8)
        mean = E_[:, 1024:1152]
        var = E_[:, 1152:1280]
        S.c("act", lambda e: e.activation(out=mean[:, :nt], in_=bm[:, :nt], func=AF.Identity), r=[bmn], w=["E2"])
        S.c("dve", lambda e: e.tensor_tensor(out=var[:, :nt], in0=mean[:, :nt], in1=mean[:, :nt], op=ALU.mult), r=["E2"], w=["E2"])
        S.c("dve", lambda e: e.tensor_tensor(out=var[:, :nt], in0=bq[:, :nt], in1=var[:, :nt], op=ALU.subtract), r=[bqn, "E2"], w=["E2"])
        S.c("act", lambda e: e.activation(out=var[:, :nt], in_=var[:, :nt], func=AF.Sqrt, bias=EPS), r=["E2", "CST"], w=["E2"])
        S.c("dve", lambda e: e.reciprocal(out=var[:, :nt], in_=var[:, :nt]), r=["E2"], w=["E2"])
        S.c("dve", lambda e: e.tensor_tensor(out=cs[:, :, :nt], in0=cacc[:, :, :nt], in1=mean[:, :nt].unsqueeze(1).broadcast_to([128, 8, nt]), op=ALU.subtract), r=["R", "E2"], w=["E"])
        S.c("dve", lambda e: e.tensor_tensor(out=cs[:, :, :nt], in0=cs[:, :, :nt], in1=var[:, :nt].unsqueeze(1).broadcast_to([128, 8, nt]), op=ALU.mult), r=["E", "E2"], w=["E"])
        for j in range(8):
            S.c("act", lambda e, j=j: e.activation(out=cs[:, j, :nt], in_=cs[:, j, :nt], func=AF.Silu, bias=chp[:, C_LB + j:C_LB + j + 1], scale=chp[:, C_LG + j:C_LG + j + 1]), r=["E", cn], w=["E"])
        for cb in range(4):
            wbt, wn = wload(w_co_d[l, :, cb * 256:(cb + 1) * 256], 256, b_co_d[l:l + 1, cb * 256:(cb + 1) * 256])
            bt, bn = nextbank()
            for k in range(8):
                S.c("pe", lambda e, k=k, bt=bt, wbt=wbt: e.matmul(bt[:nt, 0:256], cs[:, k, :nt], wbt[:, k, :], start=(k == 0), stop=False), r=[wn, "E"], w=[bn], quiet=True)
            S.c("pe", lambda e, bt=bt, wbt=wbt: e.matmul(bt[:nt, 0:256], cst(K_ONE)[0:1, :nt], wbt[0:1, 8, :], start=False, stop=True), r=[wn + "b", "CST"], w=[bn])
            S.c("dve", lambda e, cb=cb, bt=bt: e.tensor_tensor(out=gates[:nt, cb * 256:(cb + 1) * 256], in0=bt[:nt, 0:256], in1=gates[:nt, cb * 256:(cb + 1) * 256], op=ALU.mult), r=[bn, "G"], w=["G"])

        mark_b = len(S.ops)
        bank_pool[:] = [2, 3]
        if not early:
            for j in range(32):
                conv_b_tile(j)
        if not samp:
            S.c("pool", lambda e: e.tensor_copy(histM[l][:], mcat[:, :, 0, L:L + 3]), r=ALLMCAT, w=["histM%d" % l])
        S.c("act", lambda e: e.activation(out=xcx[:, :, :nt], in_=xcx[:, :, :nt], func=AF.Silu), r=["T"], w=["T"])
        S.c("act", lambda e: e.activation(out=xcb[:, :, :nt], in_=xcb[:, :, :nt], func=AF.Silu), r=["C"], w=["C"])
        BT = lambda g: xcb[:, g, :nt]
        CT = lambda g: xcb[:, 8 + g, :nt]
        xs_tm = X_
        B_tm = Bc_[:, 0:1024]
        cbm = Bc_[:, 1024:2048].rearrange("p (g t) -> p g t", g=8)

        def evac_xs(b0, n, bt, bn):
            S.c("act", lambda e: e.activation(out=xs_tm[:nt, b0 * 128:(b0 + n) * 128], in_=bt[:nt, 0:n * 128], func=AF.Identity), r=[bn], w=["X"])
        transpose_into(evac_xs, lambda j: xcx[:, j, :nt], 16, 128, nt, ["X"], ["T"])

        def evac_b(b0, n, bt, bn):
            S.c("act", lambda e: e.activation(out=B_tm[:nt, b0 * 128:(b0 + n) * 128], in_=bt[:nt, 0:n * 128], func=AF.Identity), r=[bn], w=["Bc"])
        transpose_into(evac_b, lambda j: xcb[:, j, :nt], 8, 128, nt, ["Bc"], ["C"])

        a_ = SM[:, 96:128]
        ew = SM[:, 128:192]
        S.c("dve", lambda e: e.tensor_tensor(out=a_[:nt], in0=dtr[:nt], in1=AB[l][:nt], op=ALU.mult), r=["sm_dt", "ab%d" % l], w=["sm_a"])
        bt, bn = nextbank()
        S.c("pe", lambda e, bt=bt: e.matmul(bt[:nt, 0:32], Umask[:nt, :nt], a_[:nt, :], start=True, stop=True), r=["sm_a", "CST"], w=[bn], quiet=True)
        S.c("pe", lambda e, bt=bt: e.matmul(bt[:nt, 32:64], Gseq[:nt, :nt], a_[:nt, :], start=True, stop=True), r=["sm_a", "CST"], w=[bn])
        S.c("act", lambda e, bt=bt: e.activation(out=ew[:nt], in_=bt[:nt, 0:64], func=AF.Exp), r=[bn], w=["sm_ew"])
        eacs = ew[:, 0:32]
        wend = ew[:, 32:64]
        h3 = lambda ap: ap.rearrange("p (h q) -> p h q", h=32)
        bc3 = lambda ap: ap.unsqueeze(2).broadcast_to([nt, 32, 64])
        yacc = Y_
        tmp = T_
        S.c("dve", lambda e: e.tensor_tensor(out=h3(yacc[:nt]), in0=h3(xs_tm[:nt]), in1=bc3(DK[l][:nt]), op=ALU.mult), r=["X", "dk%d" % l], w=["Y"])
        S.c("dve", lambda e: e.tensor_tensor(out=h3(xs_tm[:nt]), in0=h3(xs_tm[:nt]), in1=bc3(dtr[:nt]), op=ALU.mult), r=["X", "sm_dt"], w=["X"])
        dx = xs_tm
        S.c("dve", lambda e: e.tensor_copy(h3(tmp[:nt]), bc3(a_[:nt])), r=["sm_a", "T"], w=["T"])
        bt, bn = nextbank()
        sicol = cst(K_SI, 8) if samp else cst(K_ONE, 1)
        for j in range(16):
            S.c("pe", lambda e, j=j, bt=bt: e.matmul(bt[:, j * nseq:(j + 1) * nseq], tmp[:nt, j * 128:(j + 1) * 128], sicol[:nt, 0:nseq], start=True, stop=True), r=["T", "CST"], w=[bn], quiet=(j < 15))
        S.c("act", lambda e, bt=bt: e.activation(out=DECS[:, :, 0:nseq], in_=bt[:, 0:16 * nseq].rearrange("p (j b) -> p j b", j=16), func=AF.Exp), r=[bn], w=["decs"])
        for g in range(8):
            bt2, bn2 = bank(2 + g // 4)
            S.c("pe", lambda e, g=g, bt2=bt2: e.matmul(bt2[:nt, (g % 4) * nt:(g % 4 + 1) * nt], BT(g), CT(g), start=True, stop=True), r=["C"], w=[bn2], quiet=(g % 4 != 3))
        for hb in range(2):
            bt2, bn2 = bank(2 + hb)
            S.c("dve", lambda e, hb=hb, bt2=bt2: e.tensor_tensor(out=cbm[:nt, hb * 4:(hb + 1) * 4, :nt], in0=bt2[:nt, 0:4 * nt].rearrange("p (g t) -> p g t", g=4), in1=Umask[:nt, :nt].unsqueeze(1).broadcast_to([nt, 4, nt]), op=ALU.mult), r=[bn2, "CST"], w=["Bc2"])

        hT = H_

        def load_hT(hsrc, hname):
            for j0 in range(0, 16, 4):
                bt, bn = bank(2 + (j0 // 4) % 2)
                for q in range(4):
                    j = j0 + q
                    S.c("pe", lambda e, j=j, q=q, bt=bt: e.transpose(bt[:, q * 128:(q + 1) * 128], hsrc[:, j, :], ident), r=[hname, "CST"], w=[bn], quiet=(q < 3))
                S.c("act", lambda e, j0=j0, bt=bt: e.activation(out=hT[:, j0 * 128:(j0 + 4) * 128], in_=bt[:, :], func=AF.Identity), r=[bn], w=["H"])

        if not samp:
            load_hT(Hs[l], "h%d" % l)
            for g in range(8):
                yb, ybn = ybank(g // 2)
                S.c("pe", lambda e, g=g, yb=yb: e.matmul(yb[:nt, (g % 2) * 256:(g % 2 + 1) * 256], CT(g), hT[:, g * 256:(g + 1) * 256], start=True, stop=True), r=["C", "H"], w=[ybn], quiet=(g % 2 == 0))
        else:
            s0 = T["s0"]
            S.c("dve", lambda e: e.memset(tmp[:nt], 0.0), w=["T"])
            for b in range(nseq):
                hb_ = Hs[b % 2]
                hbn = "h%d" % (b % 2)
                S.d("sp", lambda e, b=b, hb_=hb_: e.dma_start(out=hb_[:], in_=sssm_d[l, s0 + b].rearrange("(j p) n -> p j n", p=128)), w=[hbn], key="ld" + hbn)
                load_hT(hb_, hbn)
                for g in range(8):
                    yb, ybn = ybank(g // 2)
                    S.c("pe", lambda e, g=g, yb=yb: e.matmul(yb[:nt, (g % 2) * 256:(g % 2 + 1) * 256], CT(g), hT[:, g * 256:(g + 1) * 256], start=True, stop=True), r=["C", "H"], w=[ybn], quiet=(g % 2 == 0))
                S.c("dve", lambda e, b=b: e.scalar_tensor_tensor(out=tmp[:nt], in0=PY[:nt], scalar=cst(K_SI, 8)[:nt, b:b + 1], in1=tmp[:nt], op0=ALU.mult, op1=ALU.add), r=YN + ["T", "CST"], w=["T"])
            S.c("dve", lambda e: e.tensor_tensor(out=h3(tmp[:nt]), in0=h3(tmp[:nt]), in1=bc3(eacs[:nt]), op=ALU.mult), r=["sm_ew", "T"], w=["T"])
        if not samp:
            S.c("dve", lambda e: e.tensor_tensor(out=h3(tmp[:nt]), in0=h3(PY[:nt]), in1=bc3(eacs[:nt]), op=ALU.mult), r=YN + ["sm_ew", "T"], w=["T"])
        S.c("dve", lambda e: e.tensor_tensor(out=yacc[:nt], in0=yacc[:nt], in1=tmp[:nt], op=ALU.add), r=["Y", "T"], w=["Y"])

        a_ops = S.ops[mark_a:mark_b]
        b_ops = S.ops[mark_b:]
        del S.ops[mark_a:]
        S.ops.extend(merge(b_ops, a_ops) if len(a_ops) >= len(b_ops) else merge(a_ops, b_ops))
        bank_pool[:] = [0, 1, 2, 3]
        for hg in range(4):
            rhs_a = R_[:, (hg % 2) * 1024:(hg % 2 + 1) * 1024]
            rn = "R" if hg % 2 == 0 else "R2"
            dec = E_[:, (hg % 2) * 1024:(hg % 2 + 1) * 1024]
            en = "E" if hg % 2 == 0 else "E2"
            ra3 = rhs_a[:nt, 0:8 * nt].rearrange("p (h t) -> p h t", h=8)
            S.c("dve", lambda e, hg=hg, ra3=ra3: e.tensor_tensor(out=ra3, in0=a_[:nt, hg * 8:(hg + 1) * 8].unsqueeze(2).broadcast_to([nt, 8, nt]), in1=Umask[:nt, :nt].unsqueeze(1).broadcast_to([nt, 8, nt]), op=ALU.mult), r=["sm_a", "CST"], w=[rn])
            for i in range(2):
                bt, bn = bank(i)
                S.c("pe", lambda e, i=i, bt=bt, rhs_a=rhs_a: e.matmul(bt[:nt, 0:4 * nt], Gmask[:nt, :nt], rhs_a[:nt, i * 4 * nt:(i + 1) * 4 * nt], start=True, stop=True), r=[rn, "CST"], w=[bn])
                S.c("act", lambda e, i=i, bt=bt, dec=dec: e.activation(out=dec[:nt, i * 4 * nt:(i + 1) * 4 * nt], in_=bt[:nt, 0:4 * nt], func=AF.Exp), r=[bn], w=[en])
            d4 = dec[:nt, 0:8 * nt].rearrange("p (g r t) -> p g r t", g=2, r=4)
            S.c("dve", lambda e, hg=hg, d4=d4: e.tensor_tensor(out=d4, in0=d4, in1=cbm[:nt, 2 * hg:2 * hg + 2, :nt].unsqueeze(2).broadcast_to([nt, 2, 4, nt]), op=ALU.mult), r=[en, "Bc2"], w=[en])
            yb, ybn = ybank(hg)
            for hh in range(8):
                head = hg * 8 + hh
                S.c("pe", lambda e, hh=hh, head=head, yb=yb, dec=dec: e.matmul(yb[:nt, hh * 64:(hh + 1) * 64], dec[:nt, hh * nt:(hh + 1) * nt], dx[:nt, head * 64:(head + 1) * 64], start=True, stop=True), r=[en, "X"], w=[ybn], quiet=(hh < 7))
        S.c("dve", lambda e: e.tensor_tensor(out=yacc[:nt], in0=yacc[:nt], in1=PY[:nt], op=ALU.add), r=["Y"] + YN, w=["Y"])

        S.c("dve", lambda e: e.tensor_tensor(out=h3(dx[:nt]), in0=h3(dx[:nt]), in1=bc3(wend[:nt]), op=ALU.mult), r=["X", "sm_ew"], w=["X"])
        Xw = dx
        for b in range(nseq):
            if samp:
                s0 = T["s0"]
                Bm = R_[:, 1024:2048]
                S.c("dve", lambda e, b=b: e.tensor_scalar(out=Bm[:nt], in0=B_tm[:nt], scalar1=cst(K_SI, 8)[:nt, b:b + 1], scalar2=None, op0=ALU.mult), r=["Bc", "CST"], w=["R2"])
                bmn_ = "R2"
                hb_ = Hs[b % 2]
                hbn = "h%d" % (b % 2)
                S.d("sp", lambda e, b=b, hb_=hb_: e.dma_start(out=hb_[:], in_=sssm_d[l, s0 + b].rearrange("(j p) n -> p j n", p=128)), w=[hbn], key="ld" + hbn)
            else:
                Bm = B_tm
                bmn_ = "Bc"
                hb_ = Hs[l]
                hbn = "h%d" % l
            for j in range(16):
                yb, ybn = ybank(j // 4)
                S.c("pe", lambda e, j=j, yb=yb, Bm=Bm: e.matmul(yb[:, (j % 4) * 128:(j % 4 + 1) * 128], Xw[:nt, j * 128:(j + 1) * 128], Bm[:nt, (j // 2) * 128:(j // 2 + 1) * 128], start=True, stop=True), r=["X", bmn_], w=[ybn], quiet=(j % 4 != 3))
            S.c("dve", lambda e, b=b, hb_=hb_: e.tensor_tensor(out=hb_[:], in0=hb_[:], in1=DECS[:, :, b:b + 1].broadcast_to([128, 16, 128]), op=ALU.mult), r=[hbn, "decs"], w=[hbn])
            S.c("dve", lambda e, hb_=hb_: e.tensor_tensor(out=hb_[:], in0=hb_[:], in1=PY[:, :].rearrange("p (j n) -> p j n", j=16), op=ALU.add), r=[hbn] + YN, w=[hbn])
            if samp:
                S.d("sp", lambda e, b=b, hb_=hb_: e.dma_start(out=ssms_d[l, s0 + b].rearrange("(j p) n -> p j n", p=128), in_=hb_[:]), r=[hbn], key="ossm")
        if (not samp) and T.get("last"):
            S.d("sp", lambda e: e.dma_start(out=ssmp_d[l].rearrange("(j p) n -> p j n", p=128), in_=Hs[l][:]), r=["h%d" % l], key="ossm")

        ssq = SM[:, 192:200]
        g8 = lambda ap: ap.rearrange("p (g q) -> p g q", g=8)
        S.c("dve", lambda e: e.tensor_tensor(out=yacc[:nt], in0=yacc[:nt], in1=zs[:nt], op=ALU.mult), r=["Y", "Z"], w=["Y"])
        S.c("dve", lambda e: e.tensor_tensor(out=tmp[:nt], in0=yacc[:nt], in1=yacc[:nt], op=ALU.mult), r=["Y", "T"], w=["T"])
        S.c("dve", lambda e: e.tensor_reduce(out=ssq[:nt], in_=g8(tmp[:nt]), axis=AX.X, op=ALU.add), r=["T"], w=["sm_ssq"])
        S.c("act", lambda e: e.activation(out=ssq[:nt], in_=ssq[:nt], func=AF.Sqrt, bias=EPS[:nt], scale=1.0 / 256.0), r=["sm_ssq", "CST"], w=["sm_ssq"])
        S.c("dve", lambda e: e.reciprocal(out=ssq[:nt], in_=ssq[:nt]), r=["sm_ssq"], w=["sm_ssq"])
        S.c("dve", lambda e: e.tensor_tensor(out=g8(yacc[:nt]), in0=g8(yacc[:nt]), in1=ssq[:nt].unsqueeze(2).broadcast_to([nt, 8, 256]), op=ALU.mult), r=["Y", "sm_ssq"], w=["Y"])
        ynT = T_.rearrange("p (j t) -> p j t", j=16)
        for j0 in range(0, 16, 4):
            bt, bn = nextbank()
            for q in range(4):
                j = j0 + q
                S.c("pe", lambda e, j=j, q=q, bt=bt: e.transpose(bt[:, q * 128:q * 128 + nt], yacc[:nt, j * 128:(j + 1) * 128], ident[:nt, :nt]), r=["Y", "CST"], w=[bn], quiet=(q < 3))
            for q in range(4):
                j = j0 + q
                S.c("act", lambda e, j=j, q=q, bt=bt: e.activation(out=ynT[:, j, :nt], in_=bt[:, q * 128:q * 128 + nt], func=AF.Identity, scale=chp[:, C_MN + j:C_MN + j + 1]), r=[bn, cn], w=["T"])
        for cb in range(4):
            bt, bn = nextbank()
            for kh in range(2):
                wbt, wn = wload(w_mo_d[l, kh * 1024:(kh + 1) * 1024, cb * 256:(cb + 1) * 256], 256)
                for k in range(8):
                    S.c("pe", lambda e, k=k, kh=kh, bt=bt, wbt=wbt: e.matmul(bt[:nt, 0:256], ynT[:, kh * 8 + k, :nt], wbt[:, k, :], start=(kh == 0 and k == 0), stop=(kh == 1 and k == 7)), r=[wn, "T"], w=[bn], quiet=(k < 7))
            S.c("dve", lambda e, cb=cb, bt=bt: e.tensor_tensor(out=gates[:nt, 1024 + cb * 256:1024 + (cb + 1) * 256], in0=bt[:nt, 0:256], in1=gates[:nt, 1024 + cb * 256:1024 + (cb + 1) * 256], op=ALU.mult), r=[bn, "G"], w=["G"])
        S.c("dve", lambda e: e.tensor_tensor(out=gates[:nt, 0:1024], in0=gates[:nt, 0:1024], in1=gates[:nt, 1024:2048], op=ALU.add), r=["G"], w=["G"])
        mT = A_[:, 1024:2048].rearrange("p (k t) -> p k t", k=8)
        for k0 in range(0, 8, 4):
            bt, bn = nextbank()
            for q in range(4):
                k = k0 + q
                S.c("pe", lambda e, k=k, q=q, bt=bt: e.transpose(bt[:, q * 128:q * 128 + nt], gates[:nt, k * 128:(k + 1) * 128], ident[:nt, :nt]), r=["G", "CST"], w=[bn], quiet=(q < 3))
            S.c("act", lambda e, k0=k0, bt=bt: e.activation(out=mT[:, k0:k0 + 4, :nt], in_=bt[:, :].rearrange("p (q t) -> p q t", q=4)[:, :, :nt], func=AF.Identity), r=[bn], w=["A2"])
        for cb in range(4):
            wbt, wn = wload(w_o_d[l, :, cb * 256:(cb + 1) * 256], 256)
            bt, bn = nextbank()
            for k in range(8):
                S.c("pe", lambda e, k=k, bt=bt, wbt=wbt: e.matmul(bt[:nt, 0:256], mT[:, k, :nt], wbt[:, k, :], start=(k == 0), stop=(k == 7)), r=[wn, "A2"], w=[bn], quiet=(k < 7))
            S.c("dve", lambda e, cb=cb, bt=bt: e.tensor_tensor(out=X[:nt, cb * 256:(cb + 1) * 256], in0=X[:nt, cb * 256:(cb + 1) * 256], in1=bt[:nt, 0:256], op=ALU.add), r=[bn, xname], w=[xname])


    def peer_route(l, T, par):
        nt = T["nt"]
        X = T["X"]
        xname = T["xname"]
        A_, X_, Y_, T_, H_, G_ = [SL[n] for n in ["A", "X", "Y", "T", "H", "G"]]
        ss = SM[:, 0:1]
        hdot = PR[:, 0, :]; wts = PR[:, 1, :]; idsf = PR[:, 2, :]
        gsm = GSMS[par]; IDS = IDSS[par]; xn2 = XN2S[par]
        xn2n = "xn2_%d" % par; idn = "ids%d" % par; gn = "pr_g%d" % par
        xn2T = A_[:, 1024:2048].rearrange("p (k t) -> p k t", k=8)
        n2w = G_[:, 0:1024]
        S.d("sp", lambda e: e.dma_start(out=n2w, in_=n2w_d[l:l + 1, :].broadcast_to([128, D])), w=["G"], key="ldG")
        S.c("dve", lambda e: e.memset(ss[:nt], 0.0), w=["sm_ss"])
        S.c("dve", lambda e: e.scalar_tensor_tensor(out=xn2[:nt], in0=X[:nt], scalar=1.0, in1=X[:nt], op0=ALU.mult, op1=ALU.mult, accum_out=ss[:nt]), r=[xname], w=[xn2n, "sm_ss"])
        S.c("act", lambda e: e.activation(out=ss[:nt], in_=ss[:nt], func=AF.Sqrt, bias=EPS[:nt], scale=1.0 / 1024.0), r=["sm_ss", "CST"], w=["sm_ss"])
        S.c("dve", lambda e: e.reciprocal(out=ss[:nt], in_=ss[:nt]), r=["sm_ss"], w=["sm_ss"])
        S.c("dve", lambda e: e.scalar_tensor_tensor(out=xn2[:nt], in0=X[:nt], scalar=ss[:nt], in1=n2w[:nt], op0=ALU.mult, op1=ALU.mult), r=[xname, "sm_ss", "G"], w=[xn2n])
        for k0 in range(0, 8, 4):
            bt, bn = nextbank()
            for q in range(4):
                k = k0 + q
                S.c("pe", lambda e, k=k, q=q, bt=bt: e.transpose(bt[:, q * 128:q * 128 + nt], xn2[:nt, k * 128:(k + 1) * 128], ident[:nt, :nt]), r=[xn2n, "CST"], w=[bn], quiet=(q < 3))
            S.c("act", lambda e, k0=k0, bt=bt: e.activation(out=xn2T[:, k0:k0 + 4, :nt], in_=bt[:, :].rearrange("p (q t) -> p q t", q=4)[:, :, :nt], func=AF.Identity), r=[bn], w=["A2"])
        qT = X_.rearrange("p (c t) -> p c t", c=16)
        keysT = Y_.rearrange("p (c k) -> p c k", c=16)
        S.d("sp", lambda e: e.dma_start(out=keysT, in_=keysT_d[l]), w=YALL, key="ldY")
        for blk in range(8):
            wbt, wn = wload(w_q_d[l, :, blk * 256:(blk + 1) * 256], 256)
            for f in range(2):
                hc = blk * 2 + f
                bt, bn = nextbank()
                for k in range(8):
                    S.c("pe", lambda e, f=f, k=k, bt=bt, wbt=wbt: e.matmul(bt[:, :nt], wbt[:, k, f * 128:(f + 1) * 128], xn2T[:, k, :nt], start=(k == 0), stop=(k == 7)), r=[wn, "A2"], w=[bn], quiet=(k < 7))
                S.c("act", lambda e, hc=hc, bt=bt: e.activation(out=qT[:, hc, :nt], in_=bt[:, :nt], func=AF.Identity), r=[bn], w=["X"])
        sc_all = T_.rearrange("p (c k) -> p c k", c=16)
        for hc in range(16):
            yb, ybn = ybank(hc // 4)
            S.c("pe", lambda e, hc=hc, yb=yb: e.matmul(yb[:nt, (hc % 4) * 128:(hc % 4 + 1) * 128], qT[:, hc, :nt], keysT[:, hc, :], start=True, stop=True), r=["X", "Y"], w=[ybn], quiet=(hc % 4 != 3))
        S.c("act", lambda e: e.activation(out=T_[:nt, :], in_=PY[:nt, :], func=AF.Identity), r=YN, w=["T"])

        def mkset(base, rtb):
            return dict(work=base[:, 0:128], cand=base[:, 256:512], candw=base[:, 512:768], idc=base[:, 768:1024], junk2=base[:, 1024:1280],
                        tv=rtb[:, 0:32], ti=rtb[:, 32:64], tiu=rtb[:, 64:96].bitcast(U32), scv=rtb[:, 96:112], esum=rtb[:, 112:113],
                        nmax=rtb[:, 113:114], i1s=rtb[:, 128:144], fiu=rtb[:, 144:160].bitcast(U32), ff=rtb[:, 160:176])

        def route_head(h, sc, sfx, hp):
            work, cand, candw, idc, junk2 = sc["work"], sc["cand"], sc["candw"], sc["idc"], sc["junk2"]
            tv, ti, tiu, scv, esum, nmax, i1s, fiu, ff = [sc[k_] for k_ in ("tv", "ti", "tiu", "scv", "esum", "nmax", "i1s", "fiu", "ff")]
            for c in range(2):
                src = sc_all[:nt, 2 * h + c, :]
                o = c * 16
                S.c("dve", lambda e, src=src, o=o: e.max(out=tv[:nt, o:o + 8], in_=src), r=["T"], w=["rt_tv" + sfx])
                S.c("dve", lambda e, src=src, o=o: e.max_index(out=tiu[:nt, o:o + 8], in_max=tv[:nt, o:o + 8], in_values=src), r=["T", "rt_tv" + sfx], w=["rt_ti" + sfx])
                S.c("dve", lambda e, src=src, o=o: e.match_replace(out=work[:nt], in_to_replace=tv[:nt, o:o + 8], in_values=src, imm_value=-1e30), r=["T", "rt_tv" + sfx], w=[hp])
                S.c("dve", lambda e, o=o: e.max(out=tv[:nt, o + 8:o + 16], in_=work[:nt]), r=[hp], w=["rt_tv" + sfx])
                S.c("dve", lambda e, o=o: e.max_index(out=tiu[:nt, o + 8:o + 16], in_max=tv[:nt, o + 8:o + 16], in_values=work[:nt]), r=[hp, "rt_tv" + sfx], w=["rt_ti" + sfx])
            S.c("dve", lambda e: e.tensor_copy(ti[:nt], tiu[:nt]), r=["rt_ti" + sfx], w=["rt_tf" + sfx])
            S.c("dve", lambda e: e.tensor_scalar(out=i1s[:nt], in0=ti[:nt, 0:16], scalar1=128.0, scalar2=None, op0=ALU.mult), r=["rt_tf" + sfx], w=["rt_i1" + sfx])
            c3 = lambda ap: ap.rearrange("p (a b) -> p a b", a=16)
            S.c("dve", lambda e: e.tensor_tensor(out=c3(cand[:nt]), in0=tv[:nt, 0:16].unsqueeze(2).broadcast_to([nt, 16, 16]), in1=tv[:nt, 16:32].unsqueeze(1).broadcast_to([nt, 16, 16]), op=ALU.add), r=["rt_tv" + sfx, hp], w=[hp + "2"])
            S.c("dve", lambda e: e.tensor_tensor(out=c3(idc[:nt]), in0=i1s[:nt].unsqueeze(2).broadcast_to([nt, 16, 16]), in1=ti[:nt, 16:32].unsqueeze(1).broadcast_to([nt, 16, 16]), op=ALU.add), r=["rt_i1" + sfx, "rt_tf" + sfx, hp], w=[hp + "3"])
            S.c("dve", lambda e: e.max(out=scv[:nt, 0:8], in_=cand[:nt]), r=[hp + "2"], w=["rt_sc" + sfx])
            S.c("dve", lambda e: e.max_index(out=fiu[:nt, 0:8], in_max=scv[:nt, 0:8], in_values=cand[:nt]), r=[hp + "2", "rt_sc" + sfx], w=["rt_fi" + sfx])
            S.c("dve", lambda e: e.match_replace(out=candw[:nt], in_to_replace=scv[:nt, 0:8], in_values=cand[:nt], imm_value=-1e30), r=[hp + "2", "rt_sc" + sfx], w=[hp + "4"])
            S.c("dve", lambda e: e.max(out=scv[:nt, 8:16], in_=candw[:nt]), r=[hp + "4"], w=["rt_sc" + sfx])
            S.c("dve", lambda e: e.max_index(out=fiu[:nt, 8:16], in_max=scv[:nt, 8:16], in_values=candw[:nt]), r=[hp + "4", "rt_sc" + sfx], w=["rt_fi" + sfx])
            S.c("dve", lambda e: e.tensor_copy(ff[:nt], fiu[:nt]), r=["rt_fi" + sfx], w=["rt_ff" + sfx])
            S.c("dve", lambda e, h=h: e.memset(idsf[:nt, h * 16:(h + 1) * 16], 0.0), w=["pr_ids%d" % h])
            for k in range(16):
                S.c("dve", lambda e, h=h, k=k: e.scalar_tensor_tensor(out=junk2[:nt], in0=cst(K_IOTA, 256)[:nt], scalar=ff[:nt, k:k + 1], in1=idc[:nt], op0=ALU.is_equal, op1=ALU.mult, accum_out=idsf[:nt, h * 16 + k:h * 16 + k + 1]), r=[hp + "3", "rt_ff" + sfx, "CST"], w=[hp + "5", "pr_ids%d" % h])
            S.c("dve", lambda e: e.tensor_scalar(out=nmax[:nt], in0=scv[:nt, 0:1], scalar1=-1.0, scalar2=None, op0=ALU.mult), r=["rt_sc" + sfx], w=["rt_nm" + sfx])
            S.c("dve", lambda e: e.memset(esum[:nt], 0.0), w=["rt_es" + sfx])
            S.c("act", lambda e, h=h: e.activation(out=gsm[:nt, h * 16:(h + 1) * 16], in_=scv[:nt], func=AF.Exp, bias=nmax[:nt], accum_out=esum[:nt]), r=["rt_sc" + sfx, "rt_es" + sfx, "rt_nm" + sfx], w=[gn + "h%d" % h, "rt_es" + sfx])
            S.c("dve", lambda e: e.reciprocal(out=esum[:nt], in_=esum[:nt]), r=["rt_es" + sfx], w=["rt_es" + sfx])
            S.c("dve", lambda e, h=h: e.tensor_scalar(out=gsm[:nt, h * 16:(h + 1) * 16], in0=gsm[:nt, h * 16:(h + 1) * 16], scalar1=esum[:nt], scalar2=None, op0=ALU.mult), r=[gn + "h%d" % h, "rt_es" + sfx], w=[gn + "h%d" % h])

        setA = mkset(H_, RT)
        setB = mkset(Y_, Y_[:, 1280:1456])
        ops_a = capture(lambda: [route_head(h, setA, "", "H") for h in (0, 2, 4, 6)])
        ops_b = capture(lambda: [route_head(h, setB, "b", "Y") for h in (1, 3, 5, 7)])
        S.ops.extend(merge(ops_a, ops_b))
        S.c("dve", lambda e: e.tensor_copy(IDS[:nt], idsf[:nt]), r=["pr_ids%d" % h for h in range(8)], w=[idn])


    def peer_gather(l, T, par):
        nt = T["nt"]
        X = T["X"]
        xname = T["xname"]
        A_, X_, Y_, T_, H_, G_ = [SL[n] for n in ["A", "X", "Y", "T", "H", "G"]]
        ss = SM[:, 0:1]
        hdot = PR[:, 0, :]; wts = PR[:, 1, :]; idsf = PR[:, 2, :]
        gsm = GSMS[par]; IDS = IDSS[par]; xn2 = XN2S[par]
        xn2n = "xn2_%d" % par; idn = "ids%d" % par; gn = "pr_g%d" % par
        ring = RING
        acc = ACC
        S.c("dve", lambda e: e.memset(hdot[:nt], 0.0), w=["pr_hd"])
        gi = [0]

        def gather(tab, hk):
            s = gi[0] % NS_RING
            gi[0] += 1
            rn = "rg%d" % s
            S.d("pool", lambda e: e.indirect_dma_start(out=ring[:nt, s, :], out_offset=None, in_=tab, in_offset=bass.IndirectOffsetOnAxis(ap=IDS[:nt, hk:hk + 1], axis=0)),
                r=[idn], w=[rn], key=rn)
            return s, rn

        for hk in range(128):
            s, rn = gather(U_d[l], hk)
            S.c("dve", lambda e, s=s, hk=hk: e.scalar_tensor_tensor(out=ring[:nt, s, :], in0=ring[:nt, s, :], scalar=1.0, in1=xn2[:nt], op0=ALU.mult, op1=ALU.mult, accum_out=hdot[:nt, hk:hk + 1]), r=[rn, xn2n], w=[rn, "pr_hd"])
        S.c("act", lambda e: e.activation(out=wts[:nt], in_=hdot[:nt], func=AF.Gelu), r=["pr_hd"], w=["pr_w"])
        S.c("dve", lambda e: e.tensor_tensor(out=wts[:nt], in0=wts[:nt], in1=gsm[:nt], op=ALU.mult), r=["pr_w"] + [gn + "h%d" % h for h in range(8)], w=["pr_w"])
        for hk in range(128):
            s, rn = gather(V_d[l], hk)
            if hk == 0:
                S.c("dve", lambda e, s=s: e.tensor_scalar(out=acc[:nt], in0=ring[:nt, s, :], scalar1=wts[:nt, 0:1], scalar2=None, op0=ALU.mult), r=[rn, "pr_w"], w=["acc"])
            else:
                S.c("dve", lambda e, s=s, hk=hk: e.scalar_tensor_tensor(out=acc[:nt], in0=ring[:nt, s, :], scalar=wts[:nt, hk:hk + 1], in1=acc[:nt], op0=ALU.mult, op1=ALU.add), r=[rn, "pr_w", "acc"], w=["acc"])
        S.c("dve", lambda e: e.tensor_tensor(out=X[:nt], in0=X[:nt], in1=acc[:nt], op=ALU.add), r=[xname, "acc"], w=[xname])

    def final_out(T):
        nt = T["nt"]
        X = T["X"]
        xname = T["xname"]
        A_ = SL["A"]; G_ = SL["G"]
        ss = SM[:, 0:1]
        fn = G_[:, 1024:2048]
        S.d("sp", lambda e: e.dma_start(out=fn, in_=fnw_d[0:1, :].broadcast_to([128, D])), w=["G"], key="ldG")
        S.c("dve", lambda e: e.memset(ss[:nt], 0.0), w=["sm_ss"])
        S.c("dve", lambda e: e.scalar_tensor_tensor(out=A_[:nt, 0:1024], in0=X[:nt], scalar=1.0, in1=X[:nt], op0=ALU.mult, op1=ALU.mult, accum_out=ss[:nt]), r=[xname], w=["A", "sm_ss"])
        S.c("act", lambda e: e.activation(out=ss[:nt], in_=ss[:nt], func=AF.Sqrt, bias=EPS[:nt], scale=1.0 / 1024.0), r=["sm_ss", "CST"], w=["sm_ss"])
        S.c("dve", lambda e: e.reciprocal(out=ss[:nt], in_=ss[:nt]), r=["sm_ss"], w=["sm_ss"])
        S.c("dve", lambda e: e.scalar_tensor_tensor(out=A_[:nt, 0:1024], in0=X[:nt], scalar=ss[:nt], in1=fn[:nt], op0=ALU.mult, op1=ALU.mult), r=[xname, "sm_ss", "G"], w=["A"])
        S.d("sp", lambda e: e.dma_start(out=T["out"], in_=A_[:nt, 0:1024]), r=["A"], key="oy")

    tiles = [dict(kind="P", nt=16, nseq=1, L=16, src=meta_d, out=None)]
    for i in range(16):
        tiles.append(dict(kind="P", nt=128, nseq=1, L=128, src=xp_d[i * 128:(i + 1) * 128, :], out=yp_d[i * 128:(i + 1) * 128, :]))
    tiles = tiles[:n_ptiles]
    if n_ptiles == 17:
        tiles[-1]["last"] = True
    if do_sample:
        for hf in range(2):
            tiles.append(dict(kind="S", nt=64, nseq=8, L=8, s0=hf * 8, src=xs_d[hf * 64:(hf + 1) * 64, :], out=ys_d[hf * 64:(hf + 1) * 64, :]))
    def capture(fn):
        saved = S.ops
        S.ops = []
        fn()
        out = S.ops
        S.ops = saved
        return out

    def merge(a, b, light_until=0, light_w=0.6):
        out = []
        na, nb = len(a), len(b)
        if na == 0 or nb == 0:
            return a + b
        wts = [light_w if i < light_until else 1.0 for i in range(nb)]
        tot = sum(wts)
        ia = 0
        acc_w = 0.0
        for ib, op in enumerate(b):
            tgt = int(acc_w * na / tot)
            while ia < tgt:
                out.append(a[ia]); ia += 1
            out.append(op)
            acc_w += wts[ib]
        out.extend(a[ia:])
        return out

    steps = []
    ptl = [T for T in tiles if T["kind"] == "P"]
    stl = [T for T in tiles if T["kind"] == "S"]
    groups = [ptl[0:1]] + [ptl[i:i + 2] for i in range(1, len(ptl), 2)] + [stl[i:i + 2] for i in range(0, len(stl), 2)]
    for grp in groups:
        for l in range(2):
            for T in grp:
                steps.append((T, l))
    for ti, T in enumerate(tiles):
        T["X"] = XB[ti % 2]
        T["xname"] = "x%d" % (ti % 2)

    def mixer_route_ops(T, l, par):
        def f():
            if l == 0:
                S.d("sp", lambda e: e.dma_start(out=T["X"][:T["nt"]], in_=T["src"]), w=[T["xname"]], key="ld" + T["xname"])
            layer_step(l, T)
            peer_route(l, T, par)
        return capture(f)

    S.ops.extend(mixer_route_ops(steps[0][0], steps[0][1], 0))
    for k, (T, l) in enumerate(steps):
        g = capture(lambda: peer_gather(l, T, k % 2))
        if k + 1 < len(steps):
            T2, l2 = steps[k + 1]
            m = mixer_route_ops(T2, l2, (k + 1) % 2)
            if T2 is not T:
                S.ops.extend(merge(g, m, light_until=T2.get("_inproj_end", 0)))
            else:
                S.ops.extend(g)
                S.ops.extend(m)
        else:
            S.ops.extend(g)
        if l == 1 and T["out"] is not None:
            final_out(T)
    S.build()
    es.close()
    return nc, S


def _pack_chp(inp, l):
    t = lambda v: np.ascontiguousarray(v.reshape(-1, 128).T)
    b_in = inp["b_in"][l]
    cw = inp["conf_dw_w"][l]
    mw = inp["m_conv_w"][l]
    cols = [t(b_in[0:2048]), t(b_in[4096:8192]),
            np.ascontiguousarray(cw.T.reshape(8, 128, 31).transpose(1, 0, 2).reshape(128, 248)),
            t(inp["conf_dw_b"][l]), t(inp["conf_ln_g"][l]), t(inp["conf_ln_b"][l]),
            np.ascontiguousarray(mw.T.reshape(32, 128, 4).transpose(1, 0, 2).reshape(128, 128)),
            t(inp["m_conv_b"][l]), t(inp["norm1_w"][l]), t(inp["m_norm_w"][l])]
    out = np.concatenate(cols, axis=1).astype(np.float32)
    assert out.shape == (128, NCH)
    return out


def make_in_maps(inp, n_cores=8):
    f = lambda a: np.ascontiguousarray(np.asarray(a, dtype=np.float32))
    inp = {k: np.asarray(v) for k, v in inp.items()}
    chp = np.stack([_pack_chp(inp, 0), _pack_chp(inp, 1)])
    keysT = f(inp["peer_keys"].reshape(2, 16, 128, 128).transpose(0, 3, 1, 2))
    shared = {
        "meta": f(inp["meta_tokens"]), "w_in": f(inp["w_in"]), "b_in": f(inp["b_in"]),
        "w_conf_out": f(inp["w_conf_out"]), "b_conf_out": f(inp["b_conf_out"]),
        "w_m_out": f(inp["w_m_out"]), "w_o": f(inp["w_o"]), "peer_w_q": f(inp["peer_w_q"]),
        "keysT": keysT, "peer_u0": f(inp["peer_u"][0]), "peer_u1": f(inp["peer_u"][1]), "peer_v0": f(inp["peer_v"][0]), "peer_v1": f(inp["peer_v"][1]),
        "chp": chp, "consts": _consts(), "norm2_w": f(inp["norm2_w"]),
        "final_norm_w": f(inp["final_norm_w"].reshape(1, D)), "A_log": f(inp["A_log"]),
        "D_skip": f(inp["D_skip"]), "dt_bias": f(inp["dt_bias"]),
    }
    maps = []
    for c in range(n_cores):
        m = dict(shared)
        m["xp"] = f(inp["x_prompt"][c])
        m["xs"] = f(inp["x_sample"][16 * c:16 * c + 16].reshape(128, D))
        m["cconf"] = f(inp["cache_conf"][:, 16 * c:16 * c + 16])
        m["cmconv"] = f(inp["cache_mconv"][:, 16 * c:16 * c + 16])
        m["sssm"] = f(inp["state_ssm"][:, 16 * c:16 * c + 16].reshape(2, 16, 2048, 128))
        maps.append(m)
    return maps


_PROG = {}


def kernel(**inputs):
    if "p" not in _PROG:
        _PROG["p"] = build_program()
    nc, _ = _PROG["p"]
    maps = make_in_maps(inputs)
    res = run_bass_kernel_spmd(nc, maps, core_ids=list(range(8)))
    R = res.results
    cat = lambda k, ax=0: np.concatenate([np.asarray(r[k], dtype=np.float32)[None] if ax is None else np.asarray(r[k], dtype=np.float32) for r in R], axis=0)
    y_prompt = np.stack([np.asarray(r["y_prompt"], np.float32) for r in R])
    y_sample = np.concatenate([np.asarray(r["y_sample"], np.float32).reshape(16, 8, D) for r in R], axis=0)
    conf_p = np.stack([np.asarray(r["conf_p"], np.float32) for r in R], axis=1)
    mconv_p = np.stack([np.asarray(r["mconv_p"], np.float32) for r in R], axis=1)
    ssm_p = np.stack([np.asarray(r["ssm_p"], np.float32).reshape(2, 32, 64, 128) for r in R], axis=1)
    conf_s = np.concatenate([np.asarray(r["conf_s"], np.float32) for r in R], axis=1)
    mconv_s = np.concatenate([np.asarray(r["mconv_s"], np.float32) for r in R], axis=1)
    ssm_s = np.concatenate([np.asarray(r["ssm_s"], np.float32).reshape(2, 16, 32, 64, 128) for r in R], axis=1)
    return (y_prompt, y_sample, conf_p, mconv_p, ssm_p, conf_s, mconv_s, ssm_s)
```
